# Optimizing a Trainium2 kernel written in Bass

```python
import math
import jax, jax.numpy as jnp
from jax import lax
import numpy as np

D_MODEL = 2048
BATCH = 8
SEQ = 4096
DEPTH = 4

D_MIX = D_MODEL
HEAD_DIM = 128
HGRN_WIDTH = D_MIX // 4
GDN_WIDTH = D_MIX // 2
POOL_WIDTH = D_MIX - HGRN_WIDTH - GDN_WIDTH
HGRN_HEADS = HGRN_WIDTH // HEAD_DIM
GDN_HEADS = GDN_WIDTH // HEAD_DIM
POOL_WINDOWS = (2, 4, 8, 16)
POOL_GROUP = POOL_WIDTH // len(POOL_WINDOWS)
POOL_MAX = max(POOL_WINDOWS)
CONV_WIDTH = 4
CHUNK = 64
D_FF = 256 * ((8 * D_MODEL // 3 + 255) // 256)
NORM_EPS = 1e-6
L2_EPS = 1e-6
IN_SPLITS = (HGRN_WIDTH,) * 4 + (GDN_WIDTH,) * 4 + (GDN_HEADS, GDN_HEADS, POOL_WIDTH)
D_IN = sum(IN_SPLITS)
IN_OFFSETS = tuple(int(o) for o in np.cumsum(IN_SPLITS)[:-1])

kernel_name = "hymba_style_hgrn2_gdn_pool_macaron"


def rms_norm(x, w):
    xf = x.astype(jnp.float32)
    y = xf * lax.rsqrt(jnp.mean(xf * xf, axis=-1, keepdims=True) + NORM_EPS)
    return (y * w.astype(jnp.float32)).astype(x.dtype)


def l2_normalize(x):
    return x * lax.rsqrt(jnp.sum(x * x, axis=-1, keepdims=True) + L2_EPS)


def swiglu(h, w_gate, w_up, w_down):
    return (jax.nn.silu(h @ w_gate) * (h @ w_up)) @ w_down


def to_chunks(a):
    b, t, h = a.shape[:3]
    a = a.reshape(b, t // CHUNK, CHUNK, h, *a.shape[3:])
    return jnp.moveaxis(a, (1, 3), (0, 2))


def from_chunks(a):
    n, b, h, c, d = a.shape
    return jnp.moveaxis(a, (0, 2), (1, 3)).reshape(b, n * c, h, d)


def causal_masks():
    idx = jnp.arange(CHUNK)
    return idx[:, None] >= idx[None, :], idx[:, None] > idx[None, :]


def hgrn2_chunked(q, k, v, log_f):
    b, t, h, dk = q.shape
    dv = v.shape[-1]
    causal, _ = causal_masks()

    def step(S, inp):
        qc, kc, vc, gc = inp
        cum = jnp.cumsum(gc, axis=-2)
        diff = cum[:, :, :, None, :] - cum[:, :, None, :, :]
        decay = jnp.exp(jnp.where(causal[:, :, None], diff, -jnp.inf))
        scores = jnp.einsum('bhtk,bhsk,bhtsk->bhts', qc, kc, decay)
        o = (jnp.einsum('bhts,bhsv->bhtv', scores, vc)
             + jnp.einsum('bhtk,bhkv->bhtv', qc * jnp.exp(cum), S))
        last = cum[:, :, -1:, :]
        S = (S * jnp.exp(last)[:, :, 0, :, None]
             + jnp.einsum('bhsk,bhsv->bhkv', kc * jnp.exp(last - cum), vc))
        return S, o

    S0 = jnp.zeros((b, h, dk, dv), jnp.float32)
    _, o = lax.scan(step, S0, (to_chunks(q), to_chunks(k), to_chunks(v), to_chunks(log_f)))
    return from_chunks(o)


def hgrn2_mixer(q, f, i, g, lb, norm_w):
    bsz, t, _ = q.shape
    heads = lambda a: a.reshape(bsz, t, HGRN_HEADS, HEAD_DIM)
    q = jax.nn.silu(q.astype(jnp.float32))
    z = f.astype(jnp.float32)
    log_f = jnp.logaddexp(jnp.log(lb), jnp.log1p(-lb) + jax.nn.log_sigmoid(z))
    k = (1.0 - lb) * jax.nn.sigmoid(-z)
    o = hgrn2_chunked(heads(q), heads(k), heads(i.astype(jnp.float32)), heads(log_f))
    o = rms_norm(o, norm_w) * jax.nn.silu(heads(g.astype(jnp.float32)))
    return o.reshape(bsz, t, HGRN_WIDTH)


def causal_depthwise_conv(u, w):
    c = u.shape[-1]
    return lax.conv_general_dilated(u, w[:, None, :], window_strides=(1,),
                                    padding=((CONV_WIDTH - 1, 0),),
                                    dimension_numbers=('NWC', 'WIO', 'NWC'),
                                    feature_group_count=c)


def gated_delta_chunked(q, k, v, g, beta):
    b, t, h, dk = q.shape
    dv = v.shape[-1]
    q, k, v = to_chunks(q), to_chunks(k), to_chunks(v)
    g, beta = to_chunks(g), to_chunks(beta)
    causal, strict = causal_masks()
    G = jnp.cumsum(g, axis=-1)
    decay = jnp.exp(jnp.where(causal, G[..., :, None] - G[..., None, :], -jnp.inf))
    kb = k * beta[..., None]
    L = jnp.where(strict, jnp.einsum('nbhik,nbhjk->nbhij', kb, k) * decay, 0.0)
    A = L + jnp.eye(CHUNK, dtype=L.dtype)
    rhs = jnp.concatenate([v * beta[..., None], kb * jnp.exp(G)[..., None]], axis=-1)
    sol = lax.linalg.triangular_solve(A, rhs, left_side=True, lower=True, unit_diagonal=True)
    u, w = sol[..., :dv], sol[..., dv:]
    attn = jnp.einsum('nbhik,nbhjk->nbhij', q, k) * decay
    q_dec = q * jnp.exp(G)[..., None]
    k_dec = k * jnp.exp(G[..., -1:] - G)[..., None]
    last = jnp.exp(G[..., -1])

    def step(S, inp):
        u_n, w_n, qd_n, kd_n, attn_n, last_n = inp
        v_new = u_n - jnp.einsum('bhck,bhkv->bhcv', w_n, S)
        o = (jnp.einsum('bhck,bhkv->bhcv', qd_n, S)
             + jnp.einsum('bhij,bhjv->bhiv', attn_n, v_new))
        S = S * last_n[..., None, None] + jnp.einsum('bhck,bhcv->bhkv', kd_n, v_new)
        return S, o

    S0 = jnp.zeros((b, h, dk, dv), jnp.float32)
    _, o = lax.scan(step, S0, (u, w, q_dec, k_dec, attn, last))
    return from_chunks(o)


def gdn_mixer(q, k, v, gate, b_logit, a_logit, conv_w, a_log, dt_bias, norm_w):
    bsz, t, _ = q.shape
    heads = lambda a: a.reshape(bsz, t, GDN_HEADS, HEAD_DIM)
    qkv = jnp.concatenate([q, k, v], axis=-1).astype(jnp.float32)
    qkv = jax.nn.silu(causal_depthwise_conv(qkv, conv_w.astype(jnp.float32)))
    q, k, v = jnp.split(qkv, 3, axis=-1)
    q = l2_normalize(heads(q)) * (HEAD_DIM ** -0.5)
    k = l2_normalize(heads(k))
    v = heads(v)
    beta = jax.nn.sigmoid(b_logit.astype(jnp.float32))
    g = -jnp.exp(a_log.astype(jnp.float32)) * jax.nn.softplus(
        a_logit.astype(jnp.float32) + dt_bias.astype(jnp.float32))
    o = gated_delta_chunked(q, k, v, g, beta)
    o = rms_norm(o, norm_w) * jax.nn.silu(heads(gate.astype(jnp.float32)))
    return o.reshape(bsz, t, GDN_WIDTH)


def pool_mixer(u, pool_w, pool_scale):
    uf = u.astype(jnp.float32)
    t = uf.shape[1]
    csum = jnp.pad(jnp.cumsum(uf, axis=1), ((0, 0), (POOL_MAX, 0), (0, 0)))
    pos = jnp.arange(t)
    outs = []
    for gi, win in enumerate(POOL_WINDOWS):
        sl = slice(gi * POOL_GROUP, (gi + 1) * POOL_GROUP)
        cg = csum[:, :, sl]
        window_sum = cg[:, POOL_MAX:] - cg[:, POOL_MAX - win:POOL_MAX - win + t]
        count = jnp.minimum(pos + 1, win).astype(jnp.float32)[None, :, None]
        m = window_sum / count - uf[:, :, sl]
        outs.append(jnp.einsum('btc,cd->btd', m, pool_w[gi].astype(jnp.float32)))
    return jnp.concatenate(outs, axis=-1) * pool_scale.astype(jnp.float32)


def hybrid_mixer(h, lb, w_in, conv_w, a_log, dt_bias, hgrn_norm_w, gdn_norm_w, pool_w, pool_scale, w_out):
    p = jnp.einsum('btd,de->bte', h, w_in)
    hq, hf, hi, hg, gq, gk, gv, gg, gb, ga, pu = jnp.split(p, IN_OFFSETS, axis=-1)
    y_a = hgrn2_mixer(hq, hf, hi, hg, lb, hgrn_norm_w)
    y_b = gdn_mixer(gq, gk, gv, gg, gb, ga, conv_w, a_log, dt_bias, gdn_norm_w)
    y_c = pool_mixer(pu, pool_w, pool_scale)
    y = jnp.concatenate([y_a, y_b, y_c], axis=-1).astype(h.dtype)
    return jnp.einsum('bte,ed->btd', y, w_out)


def setup_inputs(seed: int = 0) -> dict:
    key = jax.random.key(seed)
    ks = jax.random.split(key, 24)
    nrm = lambda k, shape, scale: jax.random.normal(k, shape, jnp.float32) * scale
    gain = lambda k, shape: 1.0 + 0.05 * jax.random.normal(k, shape, jnp.float32)
    dt = jnp.exp(jax.random.uniform(ks[10], (DEPTH, GDN_HEADS), jnp.float32,
                                    minval=math.log(1e-3), maxval=math.log(1e-1)))
    return {
        "x": nrm(ks[0], (BATCH, SEQ, D_MODEL), 1.0),
        "lb_logits": nrm(ks[1], (DEPTH, HGRN_WIDTH), 0.5),
        "norm_ffn1": gain(ks[2], (DEPTH, D_MODEL)),
        "ffn1_w_gate": nrm(ks[3], (DEPTH, D_MODEL, D_FF), D_MODEL ** -0.5),
        "ffn1_w_up": nrm(ks[4], (DEPTH, D_MODEL, D_FF), D_MODEL ** -0.5),
        "ffn1_w_down": nrm(ks[5], (DEPTH, D_FF, D_MODEL), D_FF ** -0.5),
        "norm_mix": gain(ks[6], (DEPTH, D_MODEL)),
        "w_in": nrm(ks[7], (DEPTH, D_MODEL, D_IN), D_MODEL ** -0.5),
        "gdn_conv_w": nrm(ks[8], (DEPTH, CONV_WIDTH, 3 * GDN_WIDTH), CONV_WIDTH ** -0.5),
        "gdn_a_log": jnp.log(jax.random.uniform(ks[9], (DEPTH, GDN_HEADS), jnp.float32, minval=1.0, maxval=16.0)),
        "gdn_dt_bias": dt + jnp.log(-jnp.expm1(-dt)),
        "hgrn_norm_w": gain(ks[11], (DEPTH, HEAD_DIM)),
        "gdn_norm_w": gain(ks[12], (DEPTH, HEAD_DIM)),
        "pool_w": nrm(ks[13], (DEPTH, len(POOL_WINDOWS), POOL_GROUP, POOL_GROUP), POOL_GROUP ** -0.5),
        "pool_scale": 1.0 + 0.1 * jax.random.normal(ks[14], (DEPTH, POOL_WIDTH), jnp.float32),
        "w_out": nrm(ks[15], (DEPTH, D_MIX, D_MODEL), D_MIX ** -0.5),
        "norm_ffn2": gain(ks[16], (DEPTH, D_MODEL)),
        "ffn2_w_gate": nrm(ks[17], (DEPTH, D_MODEL, D_FF), D_MODEL ** -0.5),
        "ffn2_w_up": nrm(ks[18], (DEPTH, D_MODEL, D_FF), D_MODEL ** -0.5),
        "ffn2_w_down": nrm(ks[19], (DEPTH, D_FF, D_MODEL), D_FF ** -0.5),
        "norm_final": gain(ks[20], (D_MODEL,)),
    }


def reference(x, lb_logits, norm_ffn1, ffn1_w_gate, ffn1_w_up, ffn1_w_down, norm_mix, w_in,
              gdn_conv_w, gdn_a_log, gdn_dt_bias, hgrn_norm_w, gdn_norm_w, pool_w, pool_scale,
              w_out, norm_ffn2, ffn2_w_gate, ffn2_w_up, ffn2_w_down, norm_final):
    lbs = jnp.cumsum(jax.nn.softmax(lb_logits.astype(jnp.float32), axis=0), axis=0)
    lbs = lbs - lbs[0:1]
    for l in range(DEPTH):
        x = x + 0.5 * swiglu(rms_norm(x, norm_ffn1[l]), ffn1_w_gate[l], ffn1_w_up[l], ffn1_w_down[l])
        x = x + hybrid_mixer(rms_norm(x, norm_mix[l]), lbs[l], w_in[l], gdn_conv_w[l], gdn_a_log[l],
                             gdn_dt_bias[l], hgrn_norm_w[l], gdn_norm_w[l], pool_w[l], pool_scale[l], w_out[l])
        x = x + 0.5 * swiglu(rms_norm(x, norm_ffn2[l]), ffn2_w_gate[l], ffn2_w_up[l], ffn2_w_down[l])
    return rms_norm(x, norm_final)
```

```python
import contextlib
import math
import os
DBG = int(os.environ.get('DBG', '99'))
CONV_ENG = os.environ.get('CONV_ENG', 'dve')
import numpy as np
import concourse.bass as bass
import concourse.mybir as mybir
from concourse.bass_utils import run_bass_kernel_spmd

F32 = mybir.dt.float32
BF16 = mybir.dt.bfloat16
AF = mybir.ActivationFunctionType
ALU = mybir.AluOpType

D = 2048
NC = 16
DFF = 5632
NF = 44
DIN = 6672
DEPTH = 4
SEQ = 4096
NHH = 4
NGH = 8
EPS = 1e-6
SEM_LIM = 30000
NEG = -30000.0


class Op:
    __slots__ = ("eng", "fn", "waits", "inc", "dma_key", "seq")

    def __init__(self, eng, fn, dma_key):
        self.eng = eng
        self.fn = fn
        self.waits = []
        self.inc = False
        self.dma_key = dma_key
        self.seq = 0


class Sched:
    ENGS = ("pe", "act", "dve", "pool", "sp")

    def __init__(self):
        self.ops = {e: [] for e in self.ENGS}
        self.last_w = {}
        self.readers = {}
        self.clock = {e: {} for e in self.ENGS}
        self.snap = {}
        self.dma_count = {}
        self.nops = 0

    def add(self, eng, fn, reads=(), writes=(), dma_key=None):
        op = Op(eng, fn, dma_key)
        is_dma = dma_key is not None
        if is_dma:
            n = self.dma_count.get(dma_key, 0) + 1
            self.dma_count[dma_key] = n
            mysrc = "dma:" + dma_key
            myseq = n
        else:
            mysrc = eng
            myseq = len(self.ops[eng]) + 1
        op.seq = len(self.ops[eng]) + 1
        deps = {}
        last_w = self.last_w
        readers = self.readers
        for k in reads:
            lw = last_w.get(k)
            if lw is not None:
                if deps.get(lw[0], 0) < lw[1]:
                    deps[lw[0]] = lw[1]
        for k in writes:
            lw = last_w.get(k)
            if lw is not None and (is_dma or lw[0] != mysrc):
                if deps.get(lw[0], 0) < lw[1]:
                    deps[lw[0]] = lw[1]
            rd = readers.get(k)
            if rd:
                for src, sq in rd.items():
                    if is_dma or src != mysrc:
                        if deps.get(src, 0) < sq:
                            deps[src] = sq
        clk = self.clock[eng]
        for src, sq in deps.items():
            if clk.get(src, 0) >= sq:
                continue
            op.waits.append((src, sq))
            if not src.startswith("dma:"):
                self.ops[src][sq - 1].inc = True
            sn = self.snap.get((src, sq))
            if sn:
                for a, b in sn.items():
                    if clk.get(a, 0) < b:
                        clk[a] = b
            clk[src] = sq
        self.ops[eng].append(op)
        self.snap[(mysrc, myseq)] = dict(clk)
        for k in reads:
            rd = readers.get(k)
            if rd is None:
                readers[k] = {mysrc: myseq}
            else:
                rd[mysrc] = myseq
        for k in writes:
            last_w[k] = (mysrc, myseq)
            readers[k] = {}
        self.nops += 1
        return op

    def emit(self, nc, final_waits):
        inc_prefix = {}
        n_sems = {}
        for e in self.ENGS:
            cnt = 0
            pref = []
            for op in self.ops[e]:
                if op.inc and op.dma_key is None:
                    cnt += 1
                pref.append(cnt)
            inc_prefix[e] = pref
            n_sems[e] = (cnt + SEM_LIM - 1) // SEM_LIM
        dma_lim = SEM_LIM // 16
        with contextlib.ExitStack() as st:
            sems = {}
            for e in self.ENGS:
                sems[e] = [st.enter_context(nc.semaphore(f"s_{e}_{i}")) for i in range(n_sems[e])]
            for k, n in self.dma_count.items():
                ns = (n + dma_lim - 1) // dma_lim
                sems["dma:" + k] = [st.enter_context(nc.semaphore(f"d_{k}_{i}")) for i in range(ns)]

            def wait_target(src, sq):
                if src.startswith("dma:"):
                    ep = (sq - 1) // dma_lim
                    return sems[src][ep], ((sq - 1) % dma_lim + 1) * 16
                c = inc_prefix[src][sq - 1]
                ep = (c - 1) // SEM_LIM
                return sems[src][ep], (c - 1) % SEM_LIM + 1

            def run(e, eng):
                cnt = 0
                for op in self.ops[e]:
                    for src, sq in op.waits:
                        sem, val = wait_target(src, sq)
                        eng.wait_ge(sem, val)
                    ins = op.fn(eng)
                    if op.dma_key is not None:
                        n = op_dma_n[id(op)]
                        ins.then_inc(sems["dma:" + op.dma_key][(n - 1) // dma_lim], 16)
                    elif op.inc:
                        ins.then_inc(sems[e][cnt // SEM_LIM], 1)
                        cnt += 1
                for src, sq in final_waits.get(e, ()):
                    sem, val = wait_target(src, sq)
                    eng.wait_ge(sem, val)

            op_dma_n = {}
            cnts = {}
            for e in self.ENGS:
                for op in self.ops[e]:
                    if op.dma_key is not None:
                        cnts[op.dma_key] = cnts.get(op.dma_key, 0) + 1
                        op_dma_n[id(op)] = cnts[op.dma_key]

            with nc.Block() as block:
                @block.tensor
                def _(eng):
                    run("pe", eng)

                @block.scalar
                def _(eng):
                    run("act", eng)

                @block.vector
                def _(eng):
                    run("dve", eng)

                @block.gpsimd
                def _(eng):
                    run("pool", eng)

                @block.sync
                def _(eng):
                    run("sp", eng)


WIN_SLABS = []
for _g in range(0, 4, 2):
    WIN_SLABS += [6160 + 128 * _g, 6160 + 128 * (_g + 1)]
for _h in range(NHH):
    WIN_SLABS += [0 + 128 * _h, 512 + 128 * _h, 1024 + 128 * _h, 1536 + 128 * _h]
for _h in range(NGH):
    WIN_SLABS += [2048 + 128 * _h, 3072 + 128 * _h, 4096 + 128 * _h, 5120 + 128 * _h]
NSLAB = len(WIN_SLABS)
LBL = 4

def _small_map(L):
    m = {}
    o = 0
    for name, n in (("lbl", 4 * LBL), ("hnw", L), ("gnw", L), ("psc", L * 4), ("convw", L * 24 * 4),
                    ("alog", L * 32), ("dtb", L * 32)):
        m[name] = o
        o += n
    m["_n"] = o
    return m

CON = {"ident": 0, "tri": 128, "triS": 256, "maskL": 384, "maskU": 512, "m01U": 640, "ones": 768, "invc": 896, "rmA": 960, "rmB": 961, "ntri": 964, "_n": 1092}


class _Probe:
    def __init__(self):
        self.rec = None

    def __getattr__(self, name):
        def f(*args, **kw):
            self.rec = (name, args, kw)
            return None
        return f


def _free(ap):
    n = 1
    for d in ap.shape[1:]:
        n *= d
    return n


def op_cost(eng, fn, dma_key):
    pr = _Probe()
    try:
        fn(pr)
    except Exception:
        return 0.3, 0.2
    name, args, kw = pr.rec
    out = kw.get("out", args[0] if args else None)
    if dma_key is not None:
        nbytes = 128 * _free(out) * (4 if out.dtype == F32 else 2)
        return 0.4, 2.0 + nbytes / 150e3
    n = _free(out) if out is not None else 64
    if eng == "pe":
        if name == "transpose":
            return 0.28, 0.25
        lhsT = kw.get("lhsT")
        c = max(n, 64) / 2200.0 + 0.02
        if lhsT is not None and lhsT.dtype == F32:
            c *= 4
        return c, 0.25
    if eng == "act":
        return 0.3 + n / 900.0, 0.2
    if eng == "pool":
        return 0.15 + n / 600.0, 0.25
    return 0.1 + n / 640.0, 0.2


_TAGS = {}
_DELTA = float(os.environ.get('SDELTA', '0'))
_NOWAR = bool(os.environ.get('NOWAR'))
_NOWAR2 = os.environ.get('NOWAR') == '2'


def list_schedule(items):
    n = len(items)
    last_w, readers = {}, {}
    preds = [None] * n
    succs = [[] for _ in range(n)]
    for i, it in enumerate(items):
        p = set()
        for k in it[2]:
            j = last_w.get(k)
            if j is not None:
                p.add(j)
        for k in it[3]:
            j = last_w.get(k)
            if j is not None and not (_NOWAR and (_NOWAR2 or not (isinstance(k, tuple) and k[0] == "ps"))):
                p.add(j)
            rd = readers.get(k)
            if rd and not (_NOWAR and (_NOWAR2 or not (isinstance(k, tuple) and k[0] == "ps"))):
                p.update(rd)
        p.discard(i)
        preds[i] = p
        for j in p:
            succs[j].append(i)
        for k in it[2]:
            readers.setdefault(k, []).append(i)
        for k in it[3]:
            last_w[k] = i
            readers[k] = []
    cost = [(it[5], 0.2) if len(it) > 5 else op_cost(it[0], it[1], it[4]) for it in items]
    prio = [0.0] * n
    for i in range(n - 1, -1, -1):
        m = 0.0
        for j in succs[i]:
            if prio[j] > m:
                m = prio[j]
        prio[i] = cost[i][0] + cost[i][1] + m
    indeg = [len(p) for p in preds]
    ready_t = [0.0] * n
    finish = [0.0] * n
    cand = {}
    for i in range(n):
        if indeg[i] == 0:
            cand.setdefault(items[i][0], []).append(i)
    eng_free = {}
    order = []
    _dbg_starts = {}
    while len(order) < n:
        best = None
        for e, lst in cand.items():
            if not lst:
                continue
            tf = eng_free.get(e, 0.0)
            for i in lst:
                st = ready_t[i] if ready_t[i] > tf else tf
                key = (st, -prio[i], i)
                if best is None or key < best[0]:
                    best = (key, e, i)
        if _DELTA > 0:
            t0 = best[0][0]
            b2 = None
            for e, lst in cand.items():
                tf = eng_free.get(e, 0.0)
                for i in lst:
                    st = ready_t[i] if ready_t[i] > tf else tf
                    if st <= t0 + _DELTA:
                        key = (-prio[i], st, i)
                        if b2 is None or key < b2[0]:
                            b2 = (key, e, i, st)
            best = ((b2[3], 0, 0), b2[1], b2[2])
        (st, _, _), e, i = best
        cand[e].remove(i)
        order.append(i)
        _dbg_starts[i] = st
        eng_free[e] = st + cost[i][0]
        finish[i] = st + cost[i][0] + cost[i][1]
        for j in succs[i]:
            if finish[i] > ready_t[j]:
                ready_t[j] = finish[i]
            indeg[j] -= 1
            if indeg[j] == 0:
                cand.setdefault(items[j][0], []).append(j)
    if os.environ.get("SCHED_DBG") == "2":
        st_of = {}
        t_eng = {}
        lastend = 0.0
        rows = []
        for i in order:
            pass
        starts = _dbg_starts
        pe_ops = [i for i in order if items[i][0] == "pe"]
        _gapsum = {}
        prev_end = 0.0
        for i in pe_ops:
            st = starts[i]
            if st - prev_end > 0.2:
                lim0 = max(preds[i], key=lambda j: finish[j]) if preds[i] else None
                kk = "none" if lim0 is None else (items[lim0][0] + ("/ps" if any(isinstance(k, tuple) and k[0] == "ps" for k in items[lim0][2]) else ""))
                _gapsum[kk] = _gapsum.get(kk, 0.0) + st - prev_end
            if st - prev_end > 400.0:
                lim = max(preds[i], key=lambda j: finish[j]) if preds[i] else None
                def desc(j):
                    pr = _Probe()
                    try:
                        items[j][1](pr)
                        nm = pr.rec[0]
                    except Exception:
                        nm = "?"
                    return "%s:%s w=%s" % (items[j][0], nm, items[j][3][:2])
                print("  PE gap %.1f us at t=%.1f before %s ; limited by %s (fin %.1f)" % (
                    st - prev_end, st, desc(i), desc(lim) if lim is not None else None, finish[lim] if lim is not None else 0))
            prev_end = st + cost[i][0]
    if os.environ.get("SCHED_DBG") == "2":
        tg = {}
        for i in range(n):
            t = _TAGS.get(id(items[i]), "")
            a = tg.get(t)
            st_i = _dbg_starts[i]
            if a is None:
                tg[t] = [st_i, finish[i]]
            else:
                a[0] = min(a[0], st_i); a[1] = max(a[1], finish[i])
        print("  stage spans:", {k: (round(v[0]), round(v[1])) for k, v in tg.items()})
    if os.environ.get("SCHED_DBG") == "2":
        print('  PE gap time by limiting pred engine:', {k: round(v, 1) for k, v in _gapsum.items()})
    if os.environ.get("SCHED_DBG"):
        busy = {}
        for i in range(n):
            busy[items[i][0]] = busy.get(items[i][0], 0.0) + cost[i][0]
        print("SCHED makespan %.1f us busy %s n=%d" % (max(finish), {k: round(v, 1) for k, v in busy.items()}, n), flush=True)
    return order


class Tile:
    def __init__(self, ap, keys, gran=None, off=0, esz=0):
        self.ap = ap
        self.keys = keys
        self.off = off
        self.esz = esz

    def kr(self, c0, c1):
        if self.esz == 0:
            return self.keys
        g0 = (self.off + c0 * self.esz) // 1024
        g1 = (self.off + c1 * self.esz - 1) // 1024
        return [("A", g) for g in range(g0, g1 + 1)]


class Builder:
    def __init__(self, T=SEQ, depth=DEPTH, TT=512, parts=("ffn1", "mix", "ffn2"), final_norm=True,
                 mix_parts=("pool", "hgrn", "gdn")):
        self.T = T
        self.depth = depth
        self.TT = TT
        self.NTG = TT // 512
        self.FG = 22 // self.NTG
        self.NFG = NF // self.FG
        self.parts = parts
        self.mix_parts = mix_parts
        self.final_norm = final_norm
        self.s = Sched()
        self.nc = bass.Bass("TRN2", target_bir_lowering=False)
        self.st = contextlib.ExitStack()
        self.ARENA_BYTES = 80 * 1024
        self.defer = None

    def sb(self, name, shape, dt):
        return self.st.enter_context(self.nc.sbuf_tensor(name, list(shape), dt))

    def dram_in(self, name, shape, dt=F32):
        return self.nc.dram_tensor(name, list(shape), dt, kind="ExternalInput").ap()

    def keys(self, items):
        out = []
        for it in items:
            if isinstance(it, Tile):
                out.extend(it.keys)
            elif isinstance(it, list):
                out.extend(it)
            else:
                out.append(it)
        return out

    def op(self, eng, fn, r=(), w=(), dma_key=None):
        item = (eng, fn, self.keys(r), self.keys(w), dma_key)
        if self.defer is not None:
            self.defer.append(item)
            _TAGS[id(item)] = getattr(self, "tag", "")
            return None
        return self.s.add(*item)

    def wload(self, dst_flat, src32, scr, skey, slotkey, kname, first):
        if not self.use_scr:
            self.op("pool", lambda e: e.dma_start(out=dst_flat, in_=src32), w=[slotkey], dma_key=kname)
        elif first:
            self.op("pool", lambda e: e.dma_start(out=dst_flat, in_=src32), w=[slotkey], dma_key=kname)
            self.op("sp", lambda e: e.dma_start(out=scr, in_=dst_flat), r=[slotkey], w=[skey], dma_key="s" + kname)
        else:
            self.op("sp", lambda e: e.dma_start(out=dst_flat, in_=scr), r=[skey], w=[slotkey], dma_key="h" + kname)

    def areset(self):
        self.aoff = 0
        self.anames = {}

    def A(self, name, cols, dt, like=None):
        esz = 4 if dt == F32 else 2
        nbytes = cols * esz
        if like is not None:
            off, nb = self.anames[like]
            assert nbytes <= nb, (name, like)
        else:
            off = self.aoff
            self.aoff += (nbytes + 1023) // 1024 * 1024
            assert self.aoff <= self.ARENA_BYTES, (name, self.aoff)
        self.anames[name] = (off, (nbytes + 1023) // 1024 * 1024)
        ap = self.arena[:, off // 2: off // 2 + nbytes // 2]
        if dt == F32:
            ap = ap.bitcast(F32)
        keys = [("A", g) for g in range(off // 1024, (off + nbytes - 1) // 1024 + 1)]
        return Tile(ap, keys, off=off, esz=esz)

    def build(self):
        nc, s, op = self.nc, self.s, self.op
        T, TT, L = self.T, self.TT, self.depth
        NTG, FG, NFG = self.NTG, self.FG, self.NFG
        SM = _small_map(L)
        self.SM = SM
        with self.st:
            xT = self.dram_in("xT", [D, T])
            yT = nc.dram_tensor("yT", [D, T], F32, kind="ExternalOutput").ap()
            wgu_d = self.dram_in("wgu", [L, 2, NF, 128, 2 * NC * 128])
            wd_d = self.dram_in("wd", [L, 2, NC, NFG, 128, FG * 128])
            norms_d = self.dram_in("norms", [128, (3 * L + 1) * NC])
            self.win_d = self.dram_in("win", [L, NSLAB // 2, 128, 2 * NC * 128])
            self.wout_d = self.dram_in("wout", [L, NC, 128, NC * 128])
            bgw_d = self.dram_in("bgw", [128, L * NC * 16])
            poolw_d = self.dram_in("poolw", [128, L * 4 * 128])
            small_d = self.dram_in("small", [128, SM["_n"]])
            consts_d = self.dram_in("consts", [128, CON["_n"]])
            self.use_scr = (T // TT) > 1 and not os.environ.get("NOSCR")
            if self.use_scr:
                wgu_s = {(l_, w_): nc.dram_tensor(f"wgu_s{l_}_{w_}", [NF, 128, 2 * NC * 128], BF16).ap()
                         for l_ in range(L) for w_ in range(2)}
                wd_s = {(l_, w_): nc.dram_tensor(f"wd_s{l_}_{w_}", [NC, NFG, 128, FG * 128], BF16).ap()
                        for l_ in range(L) for w_ in range(2)}
                self.win_s = {l_: nc.dram_tensor(f"win_s{l_}", [NSLAB // 2, 128, 2 * NC * 128], BF16).ap()
                              for l_ in range(L)}
                self.wout_s = {l_: nc.dram_tensor(f"wout_s{l_}", [NC, 128, NC * 128], BF16).ap() for l_ in range(L)}
            xTv = xT.rearrange("(c p) t -> p c t", p=128)
            yTv = yT.rearrange("(c p) t -> p c t", p=128)

            x_sb = self.sb("x_sb", [128, NC, TT], F32)
            h_sb = self.sb("h_sb", [128, NC, TT], BF16)
            NWGU, NWD = 2, 2
            wgu_sb = [self.sb(f"wgu_sb{i}", [128, 2, NC, 128], BF16) for i in range(NWGU)]
            wd_sb = [self.sb(f"wd_sb{i}", [128, FG, 128], BF16) for i in range(NWD)]
            self.arena = self.sb("arena", [128, self.ARENA_BYTES // 2], BF16)
            arena = self.arena
            sq_sb = [self.sb(f"sq_sb{i}", [128, 512], BF16) for i in range(2)]
            def top_view(off, cols):
                ap = arena[:, off // 2: off // 2 + cols * 2].bitcast(F32)
                return ap, [("A", g) for g in range(off // 1024, (off + cols * 4 - 1) // 1024 + 1)]
            AB = self.ARENA_BYTES
            sg_sb, K_sg = [], []
            for i in range(2):
                ap_, k_ = top_view(AB - TT * 4 - (i + 1) * 2048, 512)
                sg_sb.append(ap_)
                K_sg.append(k_)
            rstd_sb, K_rstd_all = top_view(AB - TT * 4, TT)
            K_rstd = [[("A", (AB - TT * 4 + tg * 2048) // 1024), ("A", (AB - TT * 4 + tg * 2048) // 1024 + 1)]
                      for tg in range(NTG)]
            norms_sb = self.sb("norms_sb", [128, (3 * L + 1) * NC], F32)
            ones_bf = self.sb("ones_bf", [128, 128], BF16)
            epsD_sb = self.sb("epsD_sb", [128, 4], F32)
            small_sb = self.sb("small_sb", [128, SM["_n"]], F32)
            con_sb = self.sb("con_sb", [128, CON["_n"]], F32)
            ident_bf = self.sb("ident_bf", [128, 128], BF16)
            bgw_sb = self.sb("bgw_sb", [128, L, NC, 16], BF16)
            poolw_sb = self.sb("poolw_sb", [128, L, 4, 128], BF16)
            lb_sb = self.sb("lb_sb", [128, 4, LBL], F32)
            oml_sb = self.sb("oml_sb", [128, 4, LBL], F32)
            lbe_sb = self.sb("lbe_sb", [128, 4, LBL], F32)
            lbs_sb = self.sb("lbs_sb", [128, 4], F32)
            nexpA_sb = self.sb("nexpA_sb", [128, L * 32], F32)
            Sh_sb = self.sb("Sh_sb", [128, L, NHH, 128], F32)
            Sg_sb = self.sb("Sg_sb", [128, L, NGH, 128], F32)
            ctail_sb = self.sb("ctail_sb", [128, L, 24, 3], F32)
            ptail_sb = self.sb("ptail_sb", [128, L, 4, 16], F32)
            bg_sb = self.sb("bg_sb", [128, 12, 32], F32)
            ps = [self.st.enter_context(nc.psum_tensor(f"ps{i}", [128, 512], F32)) for i in range(8)]
            self.__dict__.update(dict(x_sb=x_sb, h_sb=h_sb, wgu_sb=wgu_sb, wd_sb=wd_sb, sq_sb=sq_sb, sg_sb=sg_sb,
                                      rstd_sb=rstd_sb, ones_bf=ones_bf, epsD_sb=epsD_sb, small_sb=small_sb,
                                      con_sb=con_sb, ident_bf=ident_bf, bgw_sb=bgw_sb, poolw_sb=poolw_sb,
                                      lb_sb=lb_sb, oml_sb=oml_sb, nexpA_sb=nexpA_sb, Sh_sb=Sh_sb, Sg_sb=Sg_sb,
                                      ctail_sb=ctail_sb, ptail_sb=ptail_sb, bg_sb=bg_sb, ps=ps, NWGU=NWGU, NWD=NWD))
            self.wgu_i = 0
            self.wd_i = 0

            op("sp", lambda e: e.dma_start(out=norms_sb[:, :], in_=norms_d[:, :]), w=["norms"], dma_key="norms")
            op("sp", lambda e: e.dma_start(out=small_sb[:, :], in_=small_d[:, :]), w=["small"], dma_key="small")
            op("sp", lambda e: e.dma_start(out=con_sb[:, :], in_=consts_d[:, :]), w=["con"], dma_key="con")
            op("pool", lambda e: e.dma_start(out=bgw_sb[:, :, :, :].rearrange("p l k c -> p (l k c)"), in_=bgw_d[:, :]),
               w=["bgw"], dma_key="bgw")
            op("pool", lambda e: e.dma_start(out=poolw_sb[:, :, :, :].rearrange("p l g d -> p (l g d)"),
                                             in_=poolw_d[:, :]), w=["poolw"], dma_key="poolw")
            op("dve", lambda e: e.memset(ones_bf[:, :], 1.0), w=["ones"])
            op("dve", lambda e: e.memset(epsD_sb[:, 0:1], D * EPS), w=["epsD"])
            op("dve", lambda e: e.memset(epsD_sb[:, 1:2], 128 * EPS), w=["epsD"])
            op("dve", lambda e: e.memset(epsD_sb[:, 2:3], EPS), w=["epsD"])
            op("dve", lambda e: e.memset(epsD_sb[:, 3:4], 1.0), w=["epsD"])
            op("dve", lambda e: e.tensor_scalar(out=norms_sb[:, :], in0=norms_sb[:, :], scalar1=math.sqrt(D),
                                                scalar2=None, op0=ALU.mult), r=["norms"], w=["norms"])
            op("dve", lambda e: e.tensor_copy(out=ident_bf[:, :], in_=con_sb[:, CON["ident"]:CON["ident"] + 128]),
               r=["con"], w=["identbf"])
            for nm in ("hnw", "gnw"):
                o0 = SM[nm]
                op("dve", lambda e, o0=o0: e.tensor_scalar(out=small_sb[:, o0:o0 + L], in0=small_sb[:, o0:o0 + L],
                                                          scalar1=math.sqrt(128.0), scalar2=None, op0=ALU.mult),
                   r=["small"], w=["small"])
            op("dve", lambda e: e.memset(Sh_sb[:, :, :, :].rearrange("p l h d -> p (l h d)"), 0.0),
               w=["Sh%d" % i for i in range(NHH)])
            op("dve", lambda e: e.memset(Sg_sb[:, :, :, :].rearrange("p l h d -> p (l h d)"), 0.0),
               w=["Sg%d" % i for i in range(NGH)])
            op("dve", lambda e: e.memset(ctail_sb[:, :, :, :].rearrange("p l c t -> p (l c t)"), 0.0), w=["ctail"])
            op("dve", lambda e: e.memset(ptail_sb[:, :, :, :].rearrange("p l c t -> p (l c t)"), 0.0), w=["ptail"])
            lblv = small_sb[:, SM["lbl"]:SM["lbl"] + 4 * LBL].rearrange("p (h l) -> p h l", l=LBL)
            op("act", lambda e: e.activation(out=lbe_sb[:, :, :], in_=lblv, func=AF.Exp), r=["small"], w=["lbe"])
            op("dve", lambda e: e.tensor_reduce(out=lbs_sb[:, :], in_=lbe_sb[:, :, :], axis=mybir.AxisListType.X,
                                                op=ALU.add), r=["lbe"], w=["lbs"])
            op("dve", lambda e: e.reciprocal(out=lbs_sb[:, :], in_=lbs_sb[:, :]), r=["lbs"], w=["lbs"])
            op("dve", lambda e: e.memset(lb_sb[:, :, 0], 0.0), w=["lb"])
            for li in range(1, LBL):
                op("dve", lambda e, li=li: e.tensor_tensor(out=lbe_sb[:, :, li], in0=lbe_sb[:, :, li], in1=lbs_sb[:, :],
                                                           op=ALU.mult), r=["lbe", "lbs"], w=["lbe"])
                op("dve", lambda e, li=li: e.tensor_tensor(out=lb_sb[:, :, li], in0=lb_sb[:, :, li - 1],
                                                           in1=lbe_sb[:, :, li], op=ALU.add), r=["lbe", "lb"], w=["lb"])
            op("dve", lambda e: e.tensor_scalar(out=oml_sb[:, :, :], in0=lb_sb[:, :, :], scalar1=-1.0, scalar2=1.0,
                                                op0=ALU.mult, op1=ALU.add), r=["lb"], w=["oml"])
            a0 = SM["alog"]
            op("act", lambda e: e.activation(out=nexpA_sb[:, :], in_=small_sb[:, a0:a0 + L * 32], func=AF.Exp),
               r=["small"], w=["nexpA"])
            op("dve", lambda e: e.tensor_scalar(out=nexpA_sb[:, :], in0=nexpA_sb[:, :], scalar1=-1.0, scalar2=None,
                                                op0=ALU.mult), r=["nexpA"], w=["nexpA"])

            def rms_stats():
                for tg in range(NTG):
                    tsl = slice(tg * 512, (tg + 1) * 512)
                    for c in range(NC):
                        sq = sq_sb[c % 2]
                        op("act", lambda e, sq=sq, c=c, tsl=tsl: e.activation(out=sq[:, :], in_=x_sb[:, c, tsl],
                                                                          func=AF.Square),
                           r=[("x", c, tg)], w=[("sq", c % 2)])
                        op("pe", lambda e, sq=sq, c=c: e.matmul(ps[6][:, :], lhsT=ones_bf[:, :], rhs=sq[:, :],
                                                                start=(c == 0), stop=(c == NC - 1)),
                           r=[("sq", c % 2), "ones"], w=[("ps", 6)])
                    op("act", lambda e, tsl=tsl: e.activation(out=rstd_sb[:, tsl], in_=ps[6][:, :], func=AF.Sqrt,
                                                              bias=epsD_sb[:, 0:1], scale=1.0),
                       r=[("ps", 6), "epsD"], w=[K_rstd[tg]])
                    op("dve", lambda e, tsl=tsl: e.reciprocal(out=rstd_sb[:, tsl], in_=rstd_sb[:, tsl]),
                       r=[K_rstd[tg]], w=[K_rstd[tg]])

            def rmsnorm_h(nidx):
                rms_stats()
                for tg in range(NTG):
                    tsl = slice(tg * 512, (tg + 1) * 512)
                    for c in range(NC):
                        col = nidx * NC + c
                        op("dve", lambda e, c=c, tsl=tsl, col=col: e.scalar_tensor_tensor(
                            out=h_sb[:, c, tsl], in0=x_sb[:, c, tsl], scalar=norms_sb[:, col:col + 1],
                            in1=rstd_sb[:, tsl], op0=ALU.mult, op1=ALU.mult),
                            r=[("x", c, tg), K_rstd[tg], "norms"], w=[("h", c, tg)])
            self.rmsnorm_h = rmsnorm_h

            act_v = arena[:, 0:FG * TT].rearrange("p (f t) -> p f t", t=TT)

            def akey(fi, tg):
                return ("A", (fi * TT + tg * 512) * 2 // 1024)

            def ffn(l, which):
                rmsnorm_h(3 * l + (0 if which == 0 else 2))
                for fg in range(NFG):
                    for fi in range(FG):
                        fc = fg * FG + fi
                        slot = self.wgu_i % NWGU
                        self.wgu_i += 1
                        wt = wgu_sb[slot]
                        self.wload(wt[:, :, :, :].rearrange("p a k f -> p (a k f)"), wgu_d[l, which, fc, :, :],
                                   wgu_s[(l, which)][fc, :, :] if self.use_scr else None, ("scr_gu", l, which, fc),
                                   ("wgu", slot), f"wgu{slot}", self.cur_ti == 0)
                        for tg in range(NTG):
                            tsl = slice(tg * 512, (tg + 1) * 512)
                            par = (fc * NTG + tg) % 2
                            pg, pu = ps[2 * par], ps[2 * par + 1]
                            for gu, pp in ((0, pg), (1, pu)):
                                for kc in range(NC):
                                    op("pe", lambda e, wt=wt, gu=gu, kc=kc, pp=pp, tsl=tsl: e.matmul(
                                        pp[:, :], lhsT=wt[:, gu, kc, :], rhs=h_sb[:, kc, tsl],
                                        start=(kc == 0), stop=(kc == NC - 1)),
                                        r=[("wgu", slot), ("h", kc, tg)], w=[("ps", 2 * par + gu)])
                            sg = sg_sb[par]
                            op("act", lambda e, sg=sg, pg=pg: e.activation(out=sg[:, :], in_=pg[:, :], func=AF.Silu),
                               r=[("ps", 2 * par)], w=[K_sg[par]])
                            op("dve", lambda e, sg=sg, pu=pu, fi=fi, tsl=tsl: e.tensor_tensor(
                                out=act_v[:, fi, tsl], in0=sg[:, :], in1=pu[:, :], op=ALU.mult),
                                r=[K_sg[par], ("ps", 2 * par + 1)], w=[akey(fi, tg)])
                    for dc in range(NC):
                        slot = self.wd_i % NWD
                        self.wd_i += 1
                        wt = wd_sb[slot]
                        self.wload(wt[:, :, :].rearrange("p f d -> p (f d)"), wd_d[l, which, dc, fg, :, :],
                                   wd_s[(l, which)][dc, fg, :, :] if self.use_scr else None, ("scr_d", l, which, dc, fg),
                                   ("wd", slot), f"wd{slot}", self.cur_ti == 0)
                        for tg in range(NTG):
                            tsl = slice(tg * 512, (tg + 1) * 512)
                            par = (dc * NTG + tg) % 2
                            po = ps[4 + par]
                            for fi in range(FG):
                                op("pe", lambda e, wt=wt, fi=fi, po=po, tsl=tsl: e.matmul(
                                    po[:, :], lhsT=wt[:, fi, :], rhs=act_v[:, fi, tsl],
                                    start=(fi == 0), stop=(fi == FG - 1)),
                                    r=[("wd", slot), akey(fi, tg)], w=[("ps", 4 + par)])
                            op("dve", lambda e, po=po, dc=dc, tsl=tsl: e.scalar_tensor_tensor(
                                out=x_sb[:, dc, tsl], in0=po[:, :], scalar=0.5, in1=x_sb[:, dc, tsl],
                                op0=ALU.mult, op1=ALU.add),
                                r=[("ps", 4 + par), ("x", dc, tg)], w=[("x", dc, tg)])

            ntiles = T // TT
            for ti in range(ntiles):
                t0 = ti * TT
                self.cur_ti = ti
                op("sp", lambda e, t0=t0: e.dma_start(out=x_sb[:, :, :], in_=xTv[:, :, t0:t0 + TT]),
                   w=[("x", c, tg) for c in range(NC) for tg in range(NTG)], dma_key="xin")
                for l in range(L):
                    if "ffn1" in self.parts:
                        ffn(l, 0)
                    if "mix" in self.parts:
                        for tg in range(NTG):
                            self.mixer(l, ti * NTG + tg, tg)
                    if "ffn2" in self.parts:
                        ffn(l, 1)
                if self.final_norm:
                    rms_stats()
                    for tg in range(NTG):
                        tsl = slice(tg * 512, (tg + 1) * 512)
                        for c in range(NC):
                            col = 3 * L * NC + c
                            op("dve", lambda e, c=c, tsl=tsl, col=col: e.scalar_tensor_tensor(
                                out=x_sb[:, c, tsl], in0=x_sb[:, c, tsl], scalar=norms_sb[:, col:col + 1],
                                in1=rstd_sb[:, tsl], op0=ALU.mult, op1=ALU.mult),
                                r=[("x", c, tg), K_rstd[tg], "norms"], w=[("x", c, tg)])
                op("sp", lambda e, t0=t0: e.dma_start(out=yTv[:, :, t0:t0 + TT], in_=x_sb[:, :, :]),
                   r=[("x", c, tg) for c in range(NC) for tg in range(NTG)], dma_key="yout")

            nout = s.dma_count["yout"]
            s.emit(nc, {"sp": [("dma:yout", nout)]})
        return nc

    def win_pair(self, l, pair):
        slot = self.wgu_i % self.NWGU
        self.wgu_i += 1
        wt = self.wgu_sb[slot]
        win_d = self.win_d
        self.wload(wt[:, :, :, :].rearrange("p a k f -> p (a k f)"), win_d[l, pair, :, :],
                   self.win_s[l][pair, :, :] if self.use_scr else None, ("scr_in", l, pair),
                   ("wgu", slot), f"wgu{slot}", self.cur_ti == 0)
        return wt, ("wgu", slot)

    def proj_fm(self, wt, wkey, a, bank, tg):
        ps, h_sb = self.ps, self.h_sb
        tsl = slice(tg * 512, (tg + 1) * 512)
        for kc in range(NC):
            self.op("pe", lambda e, kc=kc: e.matmul(ps[bank][:, :], lhsT=wt[:, a, kc, :], rhs=h_sb[:, kc, tsl],
                                                   start=(kc == 0), stop=(kc == NC - 1)),
                    r=[wkey, ("h", kc, tg)], w=[("ps", bank)])

    def mixer(self, l, gti, tg):
        op, ps, A = self.op, self.ps, self.A
        L = self.depth
        SM = self.SM
        small, con = self.small_sb, self.con_sb
        h_sb, x_sb = self.h_sb, self.x_sb
        tsl = slice(tg * 512, (tg + 1) * 512)
        ones_bf, ident_bf, epsD = self.ones_bf, self.ident_bf, self.epsD_sb
        sq_sb = self.sq_sb
        c_ident = con[:, CON["ident"]:CON["ident"] + 128]
        c_tri = con[:, CON["tri"]:CON["tri"] + 128]
        c_triS = con[:, CON["triS"]:CON["triS"] + 128]
        c_maskL = con[:, CON["maskL"]:CON["maskL"] + 128]
        c_maskU = con[:, CON["maskU"]:CON["maskU"] + 128]
        c_m01U = con[:, CON["m01U"]:CON["m01U"] + 128]
        c_ones = con[:, CON["ones"]:CON["ones"] + 128]
        psT = ps[7][:, :].bitcast(BF16)

        if tg == 0:
            self.rmsnorm_h(3 * l + 1)
        self.areset()
        outer_defer = []
        self.defer = outer_defer
        self.outer_defer = outer_defer
        y = A("y", NC * 512, BF16)
        yv = y.ap.rearrange("p (c t) -> p c t", t=512)

        def ykey(c):
            return y.kr(c * 512, (c + 1) * 512)

        def head_out(po_bank, gate_t, nw_col, ychunk, stat_bank=6):
            sq = sq_sb[0]
            op("act", lambda e: e.activation(out=sq[:, :], in_=ps[po_bank][:, :], func=AF.Square),
               r=[("ps", po_bank)], w=[("sq", 0)])
            op("pe", lambda e: e.matmul(ps[stat_bank][:, :], lhsT=ones_bf[:, :], rhs=sq[:, :], start=True, stop=True),
               r=[("sq", 0), "ones"], w=[("ps", stat_bank)])
            rs = self._rs
            op("act", lambda e: e.activation(out=rs.ap, in_=ps[stat_bank][:, :], func=AF.Sqrt, bias=epsD[:, 1:2], scale=1.0),
               r=[("ps", stat_bank), "epsD"], w=[rs])
            op("dve", lambda e: e.reciprocal(out=rs.ap, in_=rs.ap), r=[rs], w=[rs])
            t1 = self._t1
            op("dve", lambda e: e.scalar_tensor_tensor(out=t1.ap, in0=ps[po_bank][:, :],
                                                       scalar=small[:, nw_col:nw_col + 1], in1=rs.ap,
                                                       op0=ALU.mult, op1=ALU.mult),
               r=[("ps", po_bank), "small", rs], w=[t1])
            op("dve", lambda e: e.tensor_tensor(out=yv[:, ychunk, :], in0=t1.ap, in1=gate_t.ap, op=ALU.mult),
               r=[t1, gate_t], w=[ykey(ychunk)])

        self._rs = A("rs", 512, F32)
        self._t1 = A("t1", 512, F32)
        amark = self.aoff

        bg = self.bg_sb
        if "gdn" in self.mix_parts:
            bgw = self.bgw_sb
            for blk in range(4):
                for kc in range(NC):
                    op("pe", lambda e, blk=blk, kc=kc: e.matmul(
                        ps[4][:, blk * 16:(blk + 1) * 16], lhsT=h_sb[:, kc, tg * 512 + blk * 128: tg * 512 + (blk + 1) * 128],
                        rhs=bgw[:, l, kc, :], start=(kc == 0), stop=(kc == NC - 1)),
                        r=[("h", kc, tg), "bgw"], w=[("ps", 4)])
            pbg = ps[4][:, 0:64].rearrange("p (b c) -> p b c", c=16)
            BETA, NBETA, XA, GG, NGG, EGT, EGL, COEF = range(8)

            def bgv(i):
                return bg[:, i, :].rearrange("p (b h) -> p b h", h=8)
            op("act", lambda e: e.activation(out=bgv(BETA), in_=pbg[:, :, 0:8], func=AF.Sigmoid),
               r=[("ps", 4)], w=[("bg", BETA)])
            op("dve", lambda e: e.tensor_scalar(out=bg[:, NBETA, :], in0=bg[:, BETA, :], scalar1=-1.0, scalar2=None,
                                                op0=ALU.mult), r=[("bg", BETA)], w=[("bg", NBETA)])
            d0 = SM["dtb"] + l * 32
            op("dve", lambda e: e.tensor_tensor(out=bgv(XA), in0=pbg[:, :, 8:16],
                                                in1=small[:, d0:d0 + 32].rearrange("p (b h) -> p b h", h=8),
                                                op=ALU.add), r=[("ps", 4), "small"], w=[("bg", XA)])
            op("act", lambda e: e.activation(out=bg[:, XA, :], in_=bg[:, XA, :], func=AF.Exp),
               r=[("bg", XA)], w=[("bg", XA)])
            op("act", lambda e: e.activation(out=bg[:, XA, :], in_=bg[:, XA, :], func=AF.Ln, bias=epsD[:, 3:4], scale=1.0),
               r=[("bg", XA), "epsD"], w=[("bg", XA)])
            nA = self.nexpA_sb
            op("dve", lambda e: e.tensor_tensor(out=bg[:, GG, :], in0=bg[:, XA, :], in1=nA[:, l * 32:(l + 1) * 32],
                                                op=ALU.mult), r=[("bg", XA), "nexpA"], w=[("bg", GG)])
            op("dve", lambda e: e.tensor_scalar(out=bg[:, NGG, :], in0=bg[:, GG, :], scalar1=-1.0, scalar2=None,
                                                op0=ALU.mult), r=[("bg", GG)], w=[("bg", NGG)])
            for blk in range(4):
                op("pe", lambda e, blk=blk: e.matmul(ps[5][:, blk * 8:(blk + 1) * 8], lhsT=c_tri,
                                                     rhs=bg[:, GG, blk * 8:(blk + 1) * 8], start=True, stop=True),
                   r=["con", ("bg", GG)], w=[("ps", 5)])
                op("pe", lambda e, blk=blk: e.matmul(ps[5][:, 32 + blk * 8:32 + (blk + 1) * 8], lhsT=c_triS,
                                                     rhs=bg[:, GG, blk * 8:(blk + 1) * 8], start=True, stop=True),
                   r=["con", ("bg", GG)], w=[("ps", 5)])
            op("act", lambda e: e.activation(out=bg[:, EGT, :], in_=ps[5][:, 0:32], func=AF.Exp),
               r=[("ps", 5)], w=[("bg", EGT)])
            op("act", lambda e: e.activation(out=bg[:, EGL, :], in_=ps[5][:, 32:64], func=AF.Exp),
               r=[("ps", 5)], w=[("bg", EGL)])
            op("dve", lambda e: e.tensor_tensor(out=bg[:, COEF, :], in0=bg[:, BETA, :], in1=bg[:, EGT, :], op=ALU.mult),
               r=[("bg", BETA), ("bg", EGT)], w=[("bg", COEF)])
            op("dve", lambda e: e.tensor_scalar(out=bg[:, 8, :], in0=bg[:, EGL, :], scalar1=con[:, CON["rmB"]:CON["rmB"] + 1],
                                                scalar2=None, op0=ALU.mult), r=[("bg", EGL), "con"], w=[("bg", 8)])
            op("dve", lambda e: e.tensor_scalar(out=bg[:, EGL, :], in0=bg[:, EGL, :], scalar1=con[:, CON["rmA"]:CON["rmA"] + 1],
                                                scalar2=None, op0=ALU.mult), r=[("bg", EGL), "con"], w=[("bg", EGL)])

        if "pool" in self.mix_parts:
            ubuf = A("ubuf", 528, F32)
            sA = A("sA", 528, F32)
            sB = A("sB", 528, F32)
            m_bf = A("m_bf", 512, BF16)
            ptail = self.ptail_sb
            poolw = self.poolw_sb
            for gi in range(4):
                if gi % 2 == 0:
                    wt, wkey = self.win_pair(l, gi // 2)
                self.proj_fm(wt, wkey, gi % 2, 0, tg)
                op("dve", lambda e, gi=gi: e.tensor_copy(out=ubuf.ap[:, 0:16], in_=ptail[:, l, gi, :]),
                   r=["ptail"], w=[ubuf])
                op("act", lambda e: e.activation(out=ubuf.ap[:, 16:528], in_=ps[0][:, :], func=AF.Copy),
                   r=[("ps", 0)], w=[ubuf])
                op("dve", lambda e, gi=gi: e.tensor_copy(out=ptail[:, l, gi, :], in_=ubuf.ap[:, 512:528]),
                   r=[ubuf], w=["ptail"])
                src = ubuf
                sh = 1
                bufs = [sA, sB]
                for j in range(gi + 1):
                    dst = bufs[j % 2]
                    lo = 2 * sh - 1
                    op("dve", lambda e, src=src, dst=dst, lo=lo, sh=sh: e.tensor_tensor(
                        out=dst.ap[:, lo:528], in0=src.ap[:, lo:528], in1=src.ap[:, lo - sh:528 - sh], op=ALU.add),
                        r=[src], w=[dst])
                    src = dst
                    sh *= 2
                win = sh
                op("dve", lambda e, src=src, win=win: e.scalar_tensor_tensor(
                    out=m_bf.ap, in0=src.ap[:, 16:528], scalar=1.0 / win, in1=ubuf.ap[:, 16:528],
                    op0=ALU.mult, op1=ALU.subtract), r=[src, ubuf], w=[m_bf])
                if gti == 0:
                    i0 = CON["invc"] + gi * 16
                    tmpc = self._t1
                    op("dve", lambda e, src=src, i0=i0: e.tensor_tensor(out=tmpc.ap[:, 0:16], in0=src.ap[:, 16:32],
                                                                        in1=con[:, i0:i0 + 16], op=ALU.mult),
                       r=[src, "con"], w=[tmpc])
                    op("dve", lambda e: e.tensor_tensor(out=m_bf.ap[:, 0:16], in0=tmpc.ap[:, 0:16],
                                                        in1=ubuf.ap[:, 16:32], op=ALU.subtract),
                       r=[tmpc, ubuf], w=[m_bf])
                op("pe", lambda e, gi=gi: e.matmul(ps[1][:, :], lhsT=poolw[:, l, gi, :], rhs=m_bf.ap,
                                                   start=True, stop=True), r=["poolw", m_bf], w=[("ps", 1)])
                pc = SM["psc"] + l * 4 + gi
                op("dve", lambda e, gi=gi, pc=pc: e.tensor_scalar(out=yv[:, 12 + gi, :], in0=ps[1][:, :],
                                                                  scalar1=small[:, pc:pc + 1], scalar2=None,
                                                                  op0=ALU.mult),
                   r=[("ps", 1), "small"], w=[ykey(12 + gi)])
        else:
            for gi in range(4):
                op("dve", lambda e, gi=gi: e.memset(yv[:, 12 + gi, :], 0.0), w=[ykey(12 + gi)])

        self.aoff = amark
        if "hgrn" in self.mix_parts:
            Sh = self.Sh_sb
            hsets = []
            for par in range(2):
                hsets.append(dict(
                    q32=A(f"hq32{par}", 512, F32), f32=A(f"hf32{par}", 512, F32), k32=A(f"hk32{par}", 512, F32),
                    cum=A(f"hcum{par}", 512, F32), ecum=A(f"hecum{par}", 512, F32), gsil=A(f"hgsil{par}", 512, F32),
                    qt=A(f"hqt{par}", 512, BF16), kt=A(f"hkt{par}", 512, BF16), kdT=A(f"hkdT{par}", 512, BF16),
                    v_bf=A(f"hv{par}", 512, BF16), kdec=A(f"hkdec{par}", 512, BF16), kdecB=A(f"hkdecB{par}", 512, BF16),
                    scT=A(f"hscT{par}", 512, BF16), Sb32=A(f"hSb32{par}", 9 * 128, F32), Sbf=A(f"hSbf{par}", 8 * 128, BF16)))
            for hh in range(NHH):
                par = hh % 2
                T_ = hsets[par]
                q32, f32, k32, cum, ecum, gsil, qt, kt, kdT, v_bf, kdec, kdecB, scT, Sb32, Sbf = (T_[k] for k in (
                    "q32", "f32", "k32", "cum", "ecum", "gsil", "qt", "kt", "kdT", "v_bf", "kdec", "kdecB", "scT", "Sb32", "Sbf"))
                B0, B1, B2, B3 = (0, 1, 2, 3) if par == 0 else (4, 5, 6, 7)
                psTh = ps[B0][:, :].bitcast(BF16)
                wtA, kA = self.win_pair(l, 2 + 2 * hh)
                wtB, kB = self.win_pair(l, 2 + 2 * hh + 1)
                self.proj_fm(wtA, kA, 0, B0, tg)
                self.proj_fm(wtA, kA, 1, B1, tg)
                self.proj_fm(wtB, kB, 1, B3, tg)
                for blk in range(4):
                    for kc in range(NC):
                        op("pe", lambda e, blk=blk, kc=kc, wtB=wtB, B2=B2: e.matmul(
                            ps[B2][:, blk * 128:(blk + 1) * 128],
                            lhsT=h_sb[:, kc, tg * 512 + blk * 128: tg * 512 + (blk + 1) * 128], rhs=wtB[:, 0, kc, :],
                            start=(kc == 0), stop=(kc == NC - 1)), r=[("h", kc, tg), kB], w=[("ps", B2)])
                op("act", lambda e, q32=q32, B0=B0: e.activation(out=q32.ap, in_=ps[B0][:, :], func=AF.Silu), r=[("ps", B0)], w=[q32])
                op("act", lambda e, f32=f32, B1=B1: e.activation(out=f32.ap, in_=ps[B1][:, :], func=AF.Sigmoid), r=[("ps", B1)], w=[f32])
                op("act", lambda e, gsil=gsil, B3=B3: e.activation(out=gsil.ap, in_=ps[B3][:, :], func=AF.Silu), r=[("ps", B3)], w=[gsil])
                op("act", lambda e, v_bf=v_bf, B2=B2: e.activation(out=v_bf.ap, in_=ps[B2][:, :], func=AF.Copy), r=[("ps", B2)], w=[v_bf])
                op("dve", lambda e, hh=hh, f32=f32: e.tensor_scalar(out=f32.ap, in0=f32.ap, scalar1=self.oml_sb[:, hh, l:l + 1],
                                                                    scalar2=self.lb_sb[:, hh, l:l + 1], op0=ALU.mult, op1=ALU.add),
                   r=[f32, "oml", "lb"], w=[f32])
                op("dve", lambda e, f32=f32, k32=k32: e.tensor_scalar(out=k32.ap, in0=f32.ap, scalar1=-1.0, scalar2=1.0,
                                                                      op0=ALU.mult, op1=ALU.add), r=[f32], w=[k32])
                op("act", lambda e, f32=f32: e.activation(out=f32.ap, in_=f32.ap, func=AF.Ln), r=[f32], w=[f32])
                for c in range(8):
                    op("dve", lambda e, c=c, cum=cum, f32=f32: e.tensor_tensor_scan(
                        out=cum.ap[:, c * 64:(c + 1) * 64], data0=c_ones[:, 0:64], data1=f32.ap[:, c * 64:(c + 1) * 64],
                        initial=0.0, op0=ALU.mult, op1=ALU.add), r=[f32, "con"], w=[cum])
                op("act", lambda e, cum=cum, ecum=ecum: e.activation(out=ecum.ap, in_=cum.ap, func=AF.Exp), r=[cum], w=[ecum])
                op("dve", lambda e, qt=qt, q32=q32, ecum=ecum: e.tensor_tensor(out=qt.ap, in0=q32.ap, in1=ecum.ap, op=ALU.mult),
                   r=[q32, ecum], w=[qt])
                op("act", lambda e, cum=cum: e.activation(out=cum.ap, in_=cum.ap, func=AF.Exp, scale=-1.0), r=[cum], w=[cum])
                op("dve", lambda e, k32=k32, cum=cum: e.tensor_tensor(out=k32.ap, in0=k32.ap, in1=cum.ap, op=ALU.mult),
                   r=[k32, cum], w=[k32])
                op("act", lambda e, kt=kt, k32=k32: e.activation(out=kt.ap, in_=k32.ap, func=AF.Copy), r=[k32], w=[kt])
                for c in range(8):
                    op("dve", lambda e, c=c, kdT=kdT, k32=k32, ecum=ecum: e.tensor_scalar(
                        out=kdT.ap[:, c * 64:(c + 1) * 64], in0=k32.ap[:, c * 64:(c + 1) * 64],
                        scalar1=ecum.ap[:, c * 64 + 63:c * 64 + 64], scalar2=None, op0=ALU.mult), r=[k32, ecum], w=[kdT])
                for blk in range(4):
                    op("pe", lambda e, blk=blk, kdT=kdT, psTh=psTh: e.transpose(
                        out=psTh[:, blk * 128:(blk + 1) * 128], in_=kdT.ap[:, blk * 128:(blk + 1) * 128],
                        identity=ident_bf[:, :]), r=[kdT, "identbf"], w=[("ps", B0)])
                op("dve", lambda e, kdec=kdec, psTh=psTh: e.tensor_scalar(
                    out=kdec.ap, in0=psTh[:, 0:512], scalar1=con[:, CON["rmA"]:CON["rmA"] + 1], scalar2=None, op0=ALU.mult),
                    r=[("ps", B0), "con"], w=[kdec])
                op("dve", lambda e, kdecB=kdecB, psTh=psTh: e.tensor_scalar(
                    out=kdecB.ap, in0=psTh[:, 0:512], scalar1=con[:, CON["rmB"]:CON["rmB"] + 1], scalar2=None, op0=ALU.mult),
                    r=[("ps", B0), "con"], w=[kdecB])
                for blk in range(4):
                    op("pe", lambda e, blk=blk, kt=kt, qt=qt, B1=B1: e.matmul(
                        ps[B1][:, blk * 128:(blk + 1) * 128], lhsT=kt.ap[:, blk * 128:(blk + 1) * 128],
                        rhs=qt.ap[:, blk * 128:(blk + 1) * 128], start=True, stop=True), r=[kt, qt], w=[("ps", B1)])
                op("dve", lambda e, scT=scT, B1=B1: e.tensor_tensor(
                    out=scT.ap.rearrange("p (b t) -> p b t", t=128), in0=ps[B1][:, :].rearrange("p (b t) -> p b t", t=128),
                    in1=c_m01U.unsqueeze(1).to_broadcast([128, 4, 128]), op=ALU.mult), r=[("ps", B1), "con"], w=[scT])
                for c in range(8):
                    b = c // 2
                    bank = B2 if c < 4 else B3
                    kd = kdec if c % 2 == 0 else kdecB
                    op("pe", lambda e, c=c, b=b, bank=bank, kd=kd, v_bf=v_bf: e.matmul(
                        ps[bank][:, (c % 4) * 128:(c % 4 + 1) * 128], lhsT=kd.ap[:, b * 128:(b + 1) * 128],
                        rhs=v_bf.ap[:, b * 128:(b + 1) * 128], start=True, stop=True),
                        r=[kd, v_bf], w=[("ps", bank)])
                op("dve", lambda e, hh=hh, Sb32=Sb32: e.tensor_copy(out=Sb32.ap[:, 0:128], in_=Sh[:, l, hh, :]),
                   r=["Sh%d" % hh], w=[Sb32])
                for c in range(8):
                    bank = B2 if c < 4 else B3
                    op("dve", lambda e, c=c, bank=bank, Sb32=Sb32, ecum=ecum: e.scalar_tensor_tensor(
                        out=Sb32.ap[:, (c + 1) * 128:(c + 2) * 128], in0=Sb32.ap[:, c * 128:(c + 1) * 128],
                        scalar=ecum.ap[:, c * 64 + 63:c * 64 + 64], in1=ps[bank][:, (c % 4) * 128:(c % 4 + 1) * 128],
                        op0=ALU.mult, op1=ALU.add), r=[Sb32, ecum, ("ps", bank)], w=[Sb32])
                op("act", lambda e, Sbf=Sbf, Sb32=Sb32: e.activation(out=Sbf.ap, in_=Sb32.ap[:, 0:1024], func=AF.Copy),
                   r=[Sb32], w=[Sbf])
                op("dve", lambda e, hh=hh, Sb32=Sb32: e.tensor_copy(out=Sh[:, l, hh, :], in_=Sb32.ap[:, 1024:1152]),
                   r=[Sb32], w=["Sh%d" % hh])
                for blk in range(4):
                    op("pe", lambda e, blk=blk, v_bf=v_bf, scT=scT, B0=B0: e.matmul(
                        ps[B0][:, blk * 128:(blk + 1) * 128], lhsT=v_bf.ap[:, blk * 128:(blk + 1) * 128],
                        rhs=scT.ap[:, blk * 128:(blk + 1) * 128], start=True, stop=False), r=[v_bf, scT], w=[("ps", B0)])
                    for half in range(2):
                        c = blk * 2 + half
                        op("pe", lambda e, c=c, half=half, Sbf=Sbf, qt=qt, B0=B0: e.matmul(
                            ps[B0][:, c * 64:(c + 1) * 64], lhsT=Sbf.ap[:, c * 128:(c + 1) * 128],
                            rhs=qt.ap[:, c * 64:(c + 1) * 64], start=False, stop=(half == 1)),
                            r=[Sbf, qt], w=[("ps", B0)])
                head_out(B0, gsil, SM["hnw"] + l, hh, stat_bank=B1)
        else:
            for hh in range(NHH):
                op("dve", lambda e, hh=hh: e.memset(yv[:, hh, :], 0.0), w=[ykey(hh)])

        self.aoff = amark
        if "gdn" in self.mix_parts:
            self.gdn_heads(l, tg, yv, ykey, head_out)
        else:
            for hd in range(NGH):
                op("dve", lambda e, hd=hd: e.memset(yv[:, 4 + hd, :], 0.0), w=[ykey(4 + hd)])

        wout_d = self.wout_d
        for dc in range(NC):
            slot = self.wd_i % self.NWD
            self.wd_i += 1
            wt = self.wd_sb[slot]
            self.wload(wt[:, 0:NC, :].rearrange("p f d -> p (f d)"), wout_d[l, dc, :, :],
                       self.wout_s[l][dc, :, :] if self.use_scr else None, ("scr_out", l, dc),
                       ("wd", slot), f"wd{slot}", self.cur_ti == 0)
            par = dc % 2
            po = ps[4 + par]
            for ec in range(NC):
                op("pe", lambda e, wt=wt, ec=ec, po=po: e.matmul(po[:, :], lhsT=wt[:, ec, :], rhs=yv[:, ec, :],
                                                                 start=(ec == 0), stop=(ec == NC - 1)),
                   r=[("wd", slot), ykey(ec)], w=[("ps", 4 + par)])
            op("dve", lambda e, po=po, dc=dc: e.tensor_tensor(out=x_sb[:, dc, tsl], in0=po[:, :], in1=x_sb[:, dc, tsl],
                                                              op=ALU.add),
               r=[("ps", 4 + par), ("x", dc, tg)], w=[("x", dc, tg)])
        self.defer = None
        if os.environ.get("NOSCHED"):
            for it in outer_defer:
                self.s.add(*it)
        else:
            macros, members = [], []
            for idx, it in enumerate(outer_defer):
                if (it[0] == "pe" and members and outer_defer[members[-1][-1]][0] == "pe"
                        and members[-1][-1] == idx - 1 and outer_defer[idx - 1][3] == it[3]):
                    members[-1].append(idx)
                else:
                    members.append([idx])
            for mem in members:
                first = outer_defer[mem[0]]
                if len(mem) == 1:
                    macros.append(first)
                else:
                    rd = []
                    for i in mem:
                        rd.extend(outer_defer[i][2])
                    fns = [outer_defer[i][1] for i in mem]

                    def multi(e, fns=fns):
                        r = None
                        for f in fns:
                            r = f(e)
                        return r
                    macros.append((first[0], multi, rd, first[3], None, sum(op_cost("pe", f, None)[0] for f in fns)))
            order = list_schedule(macros)
            for mi in order:
                for i in members[mi]:
                    self.s.add(*outer_defer[i])

    def gdn_heads(self, l, tg, yv, ykey, head_out):
        op, ps, A = self.op, self.ps, self.A
        SM = self.SM
        small, con = self.small_sb, self.con_sb
        ones_bf, ident_bf, epsD = self.ones_bf, self.ident_bf, self.epsD_sb
        sq_sb = self.sq_sb
        bg = self.bg_sb
        BETA, NBETA, XA, GG, NGG, EGT, EGL, COEF, EGLB = range(9)
        c_ident = con[:, CON["ident"]:CON["ident"] + 128]
        c_tri = con[:, CON["tri"]:CON["tri"] + 128]
        c_ntri = con[:, CON["ntri"]:CON["ntri"] + 128]
        c_maskL = con[:, CON["maskL"]:CON["maskL"] + 128]
        c_maskU = con[:, CON["maskU"]:CON["maskU"] + 128]
        c_ones = con[:, CON["ones"]:CON["ones"] + 128]
        psT = ps[7][:, :].bitcast(BF16)
        ctail, Sg = self.ctail_sb, self.Sg_sb

        cb = [A(f"cb{j}", 515, F32) for j in range(3)]
        acc = [A(f"acc{j}", 512, F32) for j in range(3)]
        vT = A("vT", 512, BF16)
        rq = A("rq", 512, F32)
        rk = A("rk", 512, F32)
        qT = A("qT", 512, BF16)
        kT = A("kT", 512, BF16)
        ds = A("ds", 512, F32, like="cb0")
        dTi = A("dTi", 512, F32, like="cb1")
        EG = A("EG", 512, F32, like="cb2")
        gbc = A("gbc", 512, F32, like="acc2")
        H = []
        for par in range(2):
            H.append(dict(
                gsil=A(f"gsil{par}", 512, F32), qdT=A(f"qdT{par}", 512, BF16), kbg=A(f"kbg{par}", 512, BF16),
                kdec=A(f"kdec{par}", 512, BF16), kdecB=A(f"kdecB{par}", 512, BF16), vb=A(f"vb{par}", 512, BF16),
                M=A(f"M{par}", 512, BF16), MT=A(f"MT{par}", 512, BF16), attnT=A(f"attnT{par}", 512, BF16),
                lastc=A(f"lastc{par}", 8, F32)))
        Pb = [A("P0", 512, BF16), A("P1", 512, BF16)]
        PTb = [A("PT0", 512, BF16), A("PT1", 512, BF16)]
        Rb = [A("R0", 512, BF16), A("R1", 512, BF16)]
        u32 = A("u32", 512, F32)
        wT = A("wT", 512, BF16)
        vnew = A("vnew", 512, BF16)
        Sbf = [A("Sbf0", 128, BF16), A("Sbf1", 128, BF16)]
        sqB = A("sqB", 512, BF16)
        ctmp = A("ctmp", 512, F32)
        rs, t1 = self._rs, self._t1

        def b4(t):
            return t.ap.rearrange("p (b t) -> p b t", t=128)

        def stageA(hd):
            h = H[hd % 2]
            gsil, qdT, kbg, kdec, kdecB, vb, M, MT, attnT, lastc = (h[k] for k in (
                "gsil", "qdT", "kbg", "kdec", "kdecB", "vb", "M", "MT", "attnT", "lastc"))
            wtA, kA = self.win_pair(l, 10 + 2 * hd)
            wtB, kB = self.win_pair(l, 10 + 2 * hd + 1)
            self.proj_fm(wtA, kA, 0, 0, tg)
            self.proj_fm(wtA, kA, 1, 1, tg)
            self.proj_fm(wtB, kB, 0, 2, tg)
            self.proj_fm(wtB, kB, 1, 3, tg)
            op("act", lambda e: e.activation(out=gsil.ap, in_=ps[3][:, :], func=AF.Silu), r=[("ps", 3)], w=[gsil])
            for j in range(3):
                cc = j * 8 + hd
                op("dve", lambda e, j=j, cc=cc: e.tensor_copy(out=cb[j].ap[:, 0:3], in_=ctail[:, l, cc, :]),
                   r=["ctail"], w=[cb[j]])
                op("act", lambda e, j=j: e.activation(out=cb[j].ap[:, 3:515], in_=ps[j][:, :], func=AF.Copy),
                   r=[("ps", j)], w=[cb[j]])
                op("dve", lambda e, j=j, cc=cc: e.tensor_copy(out=ctail[:, l, cc, :], in_=cb[j].ap[:, 512:515]),
                   r=[cb[j]], w=["ctail"])
                w0 = SM["convw"] + (l * 24 + cc) * 4
                if CONV_ENG == "dve":
                    op("dve", lambda e, j=j, w0=w0: e.tensor_scalar(out=acc[j].ap, in0=cb[j].ap[:, 3:515],
                                                                    scalar1=small[:, w0 + 3:w0 + 4], scalar2=None,
                                                                    op0=ALU.mult), r=[cb[j], "small"], w=[acc[j]])
                    for tap in (2, 1, 0):
                        op("dve", lambda e, j=j, w0=w0, tap=tap: e.scalar_tensor_tensor(
                            out=acc[j].ap, in0=cb[j].ap[:, tap:tap + 512], scalar=small[:, w0 + tap:w0 + tap + 1],
                            in1=acc[j].ap, op0=ALU.mult, op1=ALU.add), r=[cb[j], "small", acc[j]], w=[acc[j]])
                else:
                    op("pool", lambda e, j=j, w0=w0: e.tensor_scalar(out=acc[j].ap, in0=cb[j].ap[:, 3:515],
                                                                     scalar1=small[:, w0 + 3:w0 + 4], scalar2=None,
                                                                     op0=ALU.mult), r=[cb[j], "small"], w=[acc[j]])
                    for tap in (2, 1, 0):
                        op("pool", lambda e, j=j, w0=w0, tap=tap: e.tensor_scalar(
                            out=ctmp.ap, in0=cb[j].ap[:, tap:tap + 512], scalar1=small[:, w0 + tap:w0 + tap + 1],
                            scalar2=None, op0=ALU.mult), r=[cb[j], "small"], w=[ctmp])
                        op("pool", lambda e, j=j: e.tensor_tensor(out=acc[j].ap, in0=acc[j].ap, in1=ctmp.ap, op=ALU.add),
                           r=[acc[j], ctmp], w=[acc[j]])
            op("act", lambda e: e.activation(out=acc[0].ap, in_=acc[0].ap, func=AF.Silu), r=[acc[0]], w=[acc[0]])
            op("act", lambda e: e.activation(out=acc[1].ap, in_=acc[1].ap, func=AF.Silu), r=[acc[1]], w=[acc[1]])
            op("act", lambda e: e.activation(out=vT.ap, in_=acc[2].ap, func=AF.Silu), r=[acc[2]], w=[vT])
            for j, rr in ((0, rq), (1, rk)):
                sq = sq_sb[j]
                op("act", lambda e, j=j, sq=sq: e.activation(out=sq[:, :], in_=acc[j].ap, func=AF.Square),
                   r=[acc[j]], w=[("sq", j)])
                op("pe", lambda e, sq=sq: e.matmul(ps[7][:, :], lhsT=ones_bf[:, :], rhs=sq[:, :], start=True, stop=True),
                   r=[("sq", j), "ones"], w=[("ps", 7)])
                op("act", lambda e, rr=rr: e.activation(out=rr.ap, in_=ps[7][:, :], func=AF.Sqrt, bias=epsD[:, 2:3],
                                                        scale=1.0), r=[("ps", 7), "epsD"], w=[rr])
                op("dve", lambda e, rr=rr: e.reciprocal(out=rr.ap, in_=rr.ap), r=[rr], w=[rr])
            for blk in range(4):
                gcol = blk * 8 + hd
                op("dve", lambda e, blk=blk, gcol=gcol: e.tensor_scalar(
                    out=gbc.ap[:, blk * 128:(blk + 1) * 128], in0=c_ones, scalar1=bg[:, GG, gcol:gcol + 1], scalar2=None,
                    op0=ALU.mult), r=["con", ("bg", GG)], w=[gbc])
            for blk in range(4):
                bs = slice(blk * 128, (blk + 1) * 128)
                op("pe", lambda e, bs=bs: e.matmul(ps[0][:, bs], lhsT=gbc.ap[:, bs], rhs=c_tri, start=True, stop=True),
                   r=[gbc, "con"], w=[("ps", 0)])
            op("act", lambda e: e.activation(out=EG.ap, in_=ps[0][:, :], func=AF.Exp), r=[("ps", 0)], w=[EG])
            op("dve", lambda e: e.tensor_copy(out=lastc.ap, in_=EG.ap.rearrange("p (c t) -> p c t", t=64)[:, :, 63]),
               r=[EG], w=[lastc])
            op("dve", lambda e: e.scalar_tensor_tensor(out=acc[0].ap, in0=acc[0].ap, scalar=128.0 ** -0.5, in1=rq.ap,
                                                       op0=ALU.mult, op1=ALU.mult), r=[acc[0], rq], w=[acc[0]])
            op("act", lambda e: e.activation(out=qT.ap, in_=acc[0].ap, func=AF.Copy), r=[acc[0]], w=[qT])
            op("dve", lambda e: e.tensor_tensor(out=qdT.ap, in0=acc[0].ap, in1=EG.ap, op=ALU.mult),
               r=[acc[0], EG], w=[qdT])
            op("dve", lambda e: e.tensor_tensor(out=kT.ap, in0=acc[1].ap, in1=rk.ap, op=ALU.mult),
               r=[acc[1], rk], w=[kT])
            for blk in range(4):
                bs = slice(blk * 128, (blk + 1) * 128)
                op("pe", lambda e, bs=bs: e.transpose(out=psT[:, bs], in_=kT.ap[:, bs], identity=ident_bf[:, :]),
                   r=[kT, "identbf"], w=[("ps", 7)])
                op("pe", lambda e, blk=blk, bs=bs: e.transpose(out=psT[:, 512 + blk * 128:512 + (blk + 1) * 128],
                                                               in_=vT.ap[:, bs], identity=ident_bf[:, :]),
                   r=[vT, "identbf"], w=[("ps", 7)])
            for blk in range(4):
                bs = slice(blk * 128, (blk + 1) * 128)
                gcol = blk * 8 + hd
                op("dve", lambda e, bs=bs, gcol=gcol: e.tensor_scalar(out=kbg.ap[:, bs], in0=psT[:, bs],
                                                                      scalar1=bg[:, COEF, gcol:gcol + 1], scalar2=None,
                                                                      op0=ALU.mult), r=[("ps", 7), ("bg", COEF)], w=[kbg])
                op("dve", lambda e, bs=bs, gcol=gcol: e.tensor_scalar(out=kdec.ap[:, bs], in0=psT[:, bs],
                                                                      scalar1=bg[:, EGL, gcol:gcol + 1], scalar2=None,
                                                                      op0=ALU.mult), r=[("ps", 7), ("bg", EGL)], w=[kdec])
                op("dve", lambda e, bs=bs, gcol=gcol: e.tensor_scalar(out=kdecB.ap[:, bs], in0=psT[:, bs],
                                                                      scalar1=bg[:, EGLB, gcol:gcol + 1], scalar2=None,
                                                                      op0=ALU.mult), r=[("ps", 7), ("bg", EGLB)], w=[kdecB])
                op("dve", lambda e, blk=blk, bs=bs, gcol=gcol: e.tensor_scalar(
                    out=vb.ap[:, bs], in0=psT[:, 512 + blk * 128:512 + (blk + 1) * 128],
                    scalar1=bg[:, BETA, gcol:gcol + 1], scalar2=None, op0=ALU.mult),
                    r=[("ps", 7), ("bg", BETA)], w=[vb])
            for blk in range(4):
                bs = slice(blk * 128, (blk + 1) * 128)
                op("pe", lambda e, bs=bs: e.matmul(ps[1][:, bs], lhsT=c_tri, rhs=gbc.ap[:, bs], start=True, stop=False),
                   r=["con", gbc], w=[("ps", 1)])
                op("pe", lambda e, bs=bs: e.matmul(ps[1][:, bs], lhsT=gbc.ap[:, bs], rhs=c_ntri, start=False, stop=False),
                   r=["con", gbc], w=[("ps", 1)])
                op("pe", lambda e, bs=bs: e.matmul(ps[1][:, bs], lhsT=c_ident, rhs=c_maskL, start=False, stop=True),
                   r=["con"], w=[("ps", 1)])
            op("act", lambda e: e.activation(out=ds.ap, in_=ps[1][:, :], func=AF.Exp), r=[("ps", 1)], w=[ds])
            for blk in range(4):
                bs = slice(blk * 128, (blk + 1) * 128)
                op("pe", lambda e, bs=bs: e.matmul(ps[2][:, bs], lhsT=gbc.ap[:, bs], rhs=c_tri, start=True, stop=False),
                   r=["con", gbc], w=[("ps", 2)])
                op("pe", lambda e, bs=bs: e.matmul(ps[2][:, bs], lhsT=c_ntri, rhs=gbc.ap[:, bs], start=False, stop=False),
                   r=["con", gbc], w=[("ps", 2)])
                op("pe", lambda e, bs=bs: e.matmul(ps[2][:, bs], lhsT=c_ident, rhs=c_maskU, start=False, stop=True),
                   r=["con"], w=[("ps", 2)])
            op("act", lambda e: e.activation(out=dTi.ap, in_=ps[2][:, :], func=AF.Exp), r=[("ps", 2)], w=[dTi])
            for blk in range(4):
                bs = slice(blk * 128, (blk + 1) * 128)
                op("pe", lambda e, bs=bs: e.matmul(ps[3][:, bs], lhsT=kT.ap[:, bs], rhs=kT.ap[:, bs], start=True, stop=True),
                   r=[kT], w=[("ps", 3)])
                op("pe", lambda e, bs=bs: e.matmul(ps[0][:, bs], lhsT=kT.ap[:, bs], rhs=qT.ap[:, bs], start=True, stop=True),
                   r=[kT, qT], w=[("ps", 0)])
            for blk in range(4):
                bs = slice(blk * 128, (blk + 1) * 128)
                gcol = blk * 8 + hd
                op("dve", lambda e, bs=bs, gcol=gcol: e.scalar_tensor_tensor(
                    out=M.ap[:, bs], in0=ps[3][:, bs], scalar=bg[:, NBETA, gcol:gcol + 1], in1=ds.ap[:, bs],
                    op0=ALU.mult, op1=ALU.mult), r=[("ps", 3), ("bg", NBETA), ds], w=[M])
            op("dve", lambda e: e.tensor_tensor(out=attnT.ap, in0=ps[0][:, :], in1=dTi.ap, op=ALU.mult),
               r=[("ps", 0), dTi], w=[attnT])
            for blk in range(4):
                bs = slice(blk * 128, (blk + 1) * 128)
                op("pe", lambda e, bs=bs: e.transpose(out=psT[:, bs], in_=M.ap[:, bs], identity=ident_bf[:, :]),
                   r=[M, "identbf"], w=[("ps", 7)])
            op("act", lambda e: e.activation(out=MT.ap, in_=psT[:, 0:512], func=AF.Copy), r=[("ps", 7)], w=[MT])

        def stageBC(hd):
            h = H[hd % 2]
            gsil, qdT, kbg, kdec, kdecB, vb, M, MT, attnT, lastc = (h[k] for k in (
                "gsil", "qdT", "kbg", "kdec", "kdecB", "vb", "M", "MT", "attnT", "lastc"))
            P, PT = M, MT
            R = Rb[0]
            op("dve", lambda e, R=R: e.tensor_tensor(out=b4(R), in0=b4(MT),
                                                     in1=c_ident.unsqueeze(1).to_broadcast([128, 4, 128]), op=ALU.add),
               r=[MT, "con"], w=[R])
            for j in range(1, 6):
                Pn, PTn, Rn = Pb[j % 2], PTb[j % 2], Rb[j % 2]
                for blk in range(4):
                    bs = slice(blk * 128, (blk + 1) * 128)
                    op("pe", lambda e, bs=bs, P=P, PT=PT: e.matmul(ps[4][:, bs], lhsT=PT.ap[:, bs], rhs=P.ap[:, bs],
                                                                   start=True, stop=True), r=[P, PT], w=[("ps", 4)])
                if j < 5:
                    for blk in range(4):
                        bs = slice(blk * 128, (blk + 1) * 128)
                        op("pe", lambda e, bs=bs, P=P, PT=PT: e.matmul(ps[5][:, bs], lhsT=P.ap[:, bs], rhs=PT.ap[:, bs],
                                                                       start=True, stop=True), r=[P, PT], w=[("ps", 5)])
                op("act", lambda e, Pn=Pn: e.activation(out=Pn.ap, in_=ps[4][:, :], func=AF.Copy), r=[("ps", 4)], w=[Pn])
                if j < 5:
                    op("dve", lambda e, PTn=PTn: e.tensor_copy(out=PTn.ap, in_=ps[5][:, :]), r=[("ps", 5)], w=[PTn])
                for blk in range(4):
                    bs = slice(blk * 128, (blk + 1) * 128)
                    op("pe", lambda e, bs=bs, Pn=Pn, R=R: e.matmul(ps[6][:, bs], lhsT=Pn.ap[:, bs], rhs=R.ap[:, bs],
                                                                   start=True, stop=True), r=[Pn, R], w=[("ps", 6)])
                op("dve", lambda e, R=R, Rn=Rn: e.tensor_tensor(out=Rn.ap, in0=ps[6][:, :], in1=R.ap, op=ALU.add),
                   r=[("ps", 6), R], w=[Rn])
                P, PT, R = Pn, PTn, Rn
            for blk in range(4):
                bs = slice(blk * 128, (blk + 1) * 128)
                op("pe", lambda e, bs=bs, R=R: e.matmul(ps[4][:, bs], lhsT=R.ap[:, bs], rhs=vb.ap[:, bs], start=True, stop=True),
                   r=[R, vb], w=[("ps", 4)])
                op("pe", lambda e, bs=bs, R=R: e.matmul(ps[5][:, bs], lhsT=kbg.ap[:, bs], rhs=R.ap[:, bs], start=True, stop=True),
                   r=[R, kbg], w=[("ps", 5)])
            op("act", lambda e: e.activation(out=u32.ap, in_=ps[4][:, :], func=AF.Copy), r=[("ps", 4)], w=[u32])
            op("dve", lambda e: e.tensor_copy(out=wT.ap, in_=ps[5][:, :]), r=[("ps", 5)], w=[wT])
            op("act", lambda e: e.activation(out=Sbf[0].ap, in_=Sg[:, l, hd, :], func=AF.Copy), r=["Sg%d" % hd], w=[Sbf[0]])
            OB, WSB, KVB, STB = 4, 6, 5, 5
            for c in range(8):
                blk, half = c // 2, c % 2
                r0 = half * 64
                cs = slice(c * 64, (c + 1) * 64)
                bs = slice(blk * 128, (blk + 1) * 128)
                Sc, Sn = Sbf[c % 2], Sbf[(c + 1) % 2]
                wcols = slice((c % 4) * 128, (c % 4 + 1) * 128)
                op("pe", lambda e, bs=bs, Sc=Sc, wcols=wcols: e.matmul(
                    ps[WSB][:, wcols], lhsT=wT.ap[:, bs], rhs=Sc.ap, start=True, stop=True), r=[wT, Sc], w=[("ps", WSB)])
                if half == 0:
                    op("dve", lambda e, bs=bs, wcols=wcols: e.tensor_tensor(
                        out=vnew.ap[:, bs], in0=u32.ap[:, bs], in1=ps[WSB][:, wcols], op=ALU.subtract),
                        r=[u32, ("ps", WSB)], w=[vnew])
                else:
                    op("dve", lambda e, bs=bs, wcols=wcols: e.tensor_tensor(
                        out=vnew.ap[64:128, bs], in0=u32.ap[64:128, bs], in1=ps[WSB][64:128, wcols], op=ALU.subtract),
                        r=[u32, ("ps", WSB)], w=[vnew])
                op("pe", lambda e, r0=r0, bs=bs, cs=cs, blk=blk: e.matmul(
                    ps[OB][:, cs], lhsT=vnew.ap[:, bs], rhs=attnT.ap[:, blk * 128 + r0: blk * 128 + r0 + 64],
                    start=True, stop=False), r=[vnew, attnT], w=[("ps", OB)])
                op("pe", lambda e, cs=cs, Sc=Sc: e.matmul(ps[OB][:, cs], lhsT=Sc.ap, rhs=qdT.ap[:, cs],
                                                          start=False, stop=True), r=[Sc, qdT], w=[("ps", OB)])
                kd = kdec if half == 0 else kdecB
                op("pe", lambda e, bs=bs, c=c, kd=kd: e.matmul(
                    ps[KVB][:, (c % 4) * 128:(c % 4 + 1) * 128], lhsT=kd.ap[:, bs], rhs=vnew.ap[:, bs],
                    start=True, stop=True), r=[kd, vnew], w=[("ps", KVB)])
                if c < 7:
                    op("dve", lambda e, c=c, Sn=Sn: e.scalar_tensor_tensor(
                        out=Sn.ap, in0=Sg[:, l, hd, :], scalar=lastc.ap[:, c:c + 1],
                        in1=ps[KVB][:, (c % 4) * 128:(c % 4 + 1) * 128], op0=ALU.mult, op1=ALU.add),
                        r=["Sg%d" % hd, lastc, ("ps", KVB)], w=[Sn])
                op("dve", lambda e, c=c: e.scalar_tensor_tensor(
                    out=Sg[:, l, hd, :], in0=Sg[:, l, hd, :], scalar=lastc.ap[:, c:c + 1],
                    in1=ps[KVB][:, (c % 4) * 128:(c % 4 + 1) * 128], op0=ALU.mult, op1=ALU.add),
                    r=["Sg%d" % hd, lastc, ("ps", KVB)], w=["Sg%d" % hd])
            op("act", lambda e: e.activation(out=sqB.ap, in_=ps[OB][:, :], func=AF.Square), r=[("ps", OB)], w=[sqB])
            op("pe", lambda e: e.matmul(ps[STB][:, :], lhsT=ones_bf[:, :], rhs=sqB.ap, start=True, stop=True),
               r=[sqB, "ones"], w=[("ps", STB)])
            op("act", lambda e: e.activation(out=rs.ap, in_=ps[STB][:, :], func=AF.Sqrt, bias=epsD[:, 1:2], scale=1.0),
               r=[("ps", STB), "epsD"], w=[rs])
            op("dve", lambda e: e.reciprocal(out=rs.ap, in_=rs.ap), r=[rs], w=[rs])
            nw = SM["gnw"] + l
            op("dve", lambda e: e.scalar_tensor_tensor(out=t1.ap, in0=ps[OB][:, :], scalar=small[:, nw:nw + 1], in1=rs.ap,
                                                       op0=ALU.mult, op1=ALU.mult), r=[("ps", OB), "small", rs], w=[t1])
            op("dve", lambda e: e.tensor_tensor(out=yv[:, 4 + hd, :], in0=t1.ap, in1=gsil.ap, op=ALU.mult),
               r=[t1, gsil], w=[ykey(4 + hd)])

        for hd in range(NGH):
            self.tag = "A%d" % hd
            stageA(hd)
            self.tag = "BC%d" % hd
            stageBC(hd)
        self.tag = "out"


def _prep_wgu(inp, L):
    wgu = np.empty((L, 2, NF, 128, 2, NC, 128), np.float32)
    for which, (g, u) in enumerate((("ffn1_w_gate", "ffn1_w_up"), ("ffn2_w_gate", "ffn2_w_up"))):
        for gu, name in enumerate((g, u)):
            w = np.asarray(inp[name])[:L]
            w = w.reshape(L, NC, 128, NF, 128)
            wgu[:, which, :, :, gu, :, :] = w.transpose(0, 3, 2, 1, 4)
    return wgu.reshape(L, 2, NF, 128, 2 * NC * 128)


def _prep_wd(inp, L, FG):
    NFG = NF // FG
    wd = np.empty((L, 2, NC, NFG, 128, FG, 128), np.float32)
    for which, name in enumerate(("ffn1_w_down", "ffn2_w_down")):
        w = np.asarray(inp[name])[:L]
        w = w.reshape(L, NFG, FG, 128, NC, 128)
        wd[:, which] = w.transpose(0, 4, 1, 3, 2, 5)
    return wd.reshape(L, 2, NC, NFG, 128, FG * 128)


def _prep_norms(inp, L):
    cols = []
    for l in range(L):
        for name in ("norm_ffn1", "norm_mix", "norm_ffn2"):
            cols.append(np.asarray(inp[name])[l])
    cols.append(np.asarray(inp["norm_final"]))
    a = np.stack(cols, 0).reshape(3 * L + 1, NC, 128)
    return np.ascontiguousarray(a.transpose(2, 0, 1).reshape(128, (3 * L + 1) * NC)).astype(np.float32)


def _prep_win(inp, L):
    w = np.asarray(inp["w_in"])[:L]
    win = np.empty((L, NSLAB // 2, 128, 2, NC, 128), np.float32)
    for sidx, c0 in enumerate(WIN_SLABS):
        blk = w[:, :, c0:c0 + 128].reshape(L, NC, 128, 128)
        win[:, sidx // 2, :, sidx % 2, :, :] = blk.transpose(0, 2, 1, 3)
    bg = w[:, :, 6144:6160].reshape(L, NC, 128, 16).transpose(2, 0, 1, 3)
    return win.reshape(L, NSLAB // 2, 128, 2 * NC * 128), np.ascontiguousarray(bg).reshape(128, L * NC * 16)


def _prep_wout(inp, L):
    w = np.asarray(inp["w_out"])[:L].reshape(L, NC, 128, NC, 128)
    return np.ascontiguousarray(w.transpose(0, 3, 2, 1, 4)).reshape(L, NC, 128, NC * 128)


def _prep_small(inp, L):
    SM = _small_map(L)
    sm = np.zeros((128, SM["_n"]), np.float32)
    lbl = np.asarray(inp["lb_logits"]).reshape(LBL, 4, 128)
    sm[:, SM["lbl"]:SM["lbl"] + 4 * LBL] = lbl.transpose(2, 1, 0).reshape(128, 4 * LBL)
    sm[:, SM["hnw"]:SM["hnw"] + L] = np.asarray(inp["hgrn_norm_w"])[:L].T
    sm[:, SM["gnw"]:SM["gnw"] + L] = np.asarray(inp["gdn_norm_w"])[:L].T
    psc = np.asarray(inp["pool_scale"])[:L].reshape(L, 4, 128)
    sm[:, SM["psc"]:SM["psc"] + 4 * L] = psc.transpose(2, 0, 1).reshape(128, 4 * L)
    cw = np.asarray(inp["gdn_conv_w"])[:L].reshape(L, 4, 24, 128)
    sm[:, SM["convw"]:SM["convw"] + L * 96] = cw.transpose(3, 0, 2, 1).reshape(128, L * 96)
    for nm, key in (("alog", "gdn_a_log"), ("dtb", "gdn_dt_bias")):
        a = np.asarray(inp[key])[:L]
        rep = np.broadcast_to(a[None, :, None, :], (128, L, 4, 8)).reshape(128, L * 32)
        sm[:, SM[nm]:SM[nm] + L * 32] = rep
    return sm


def _consts():
    c = np.zeros((128, CON["_n"]), np.float32)
    i = np.arange(128)
    same = (i[:, None] // 64) == (i[None, :] // 64)
    c[:, CON["ident"]:CON["ident"] + 128] = np.eye(128)
    c[:, CON["tri"]:CON["tri"] + 128] = same & (i[:, None] <= i[None, :])
    c[:, CON["triS"]:CON["triS"] + 128] = same & (i[:, None] > i[None, :])
    c[:, CON["maskL"]:CON["maskL"] + 128] = np.where(same & (i[:, None] > i[None, :]), 0.0, NEG)
    c[:, CON["maskU"]:CON["maskU"] + 128] = np.where(same & (i[None, :] >= i[:, None]), 0.0, NEG)
    c[:, CON["m01U"]:CON["m01U"] + 128] = same & (i[None, :] >= i[:, None])
    c[:, CON["ones"]:CON["ones"] + 128] = 1.0
    c[:64, CON["rmA"]] = 1.0
    c[:, CON["ntri"]:CON["ntri"] + 128] = -1.0 * (same & (i[:, None] <= i[None, :]))
    c[64:, CON["rmB"]] = 1.0
    t = np.arange(16)
    for gi, win in enumerate((2, 4, 8, 16)):
        c[:, CON["invc"] + gi * 16:CON["invc"] + (gi + 1) * 16] = 1.0 / np.minimum(t + 1, win)
    return c


def prep_shared(inputs, depth, bld):
    win, bgw = _prep_win(inputs, depth)
    pw = np.asarray(inputs["pool_w"])[:depth]
    poolw = np.ascontiguousarray(pw.transpose(2, 0, 1, 3)).reshape(128, depth * 4 * 128)
    return {
        "wgu": _prep_wgu(inputs, depth),
        "wd": _prep_wd(inputs, depth, bld.FG),
        "norms": _prep_norms(inputs, depth),
        "win": win,
        "bgw": bgw,
        "wout": _prep_wout(inputs, depth),
        "poolw": poolw,
        "small": _prep_small(inputs, depth),
        "consts": _consts(),
    }


def kernel(**inputs):
    x = np.asarray(inputs["x"])
    B = x.shape[0]
    bld = Builder()
    nc = bld.build()
    shared = prep_shared(inputs, DEPTH, bld)
    in_maps = []
    for b in range(B):
        m = dict(shared)
        m["xT"] = np.ascontiguousarray(x[b].T)
        in_maps.append(m)
    res = run_bass_kernel_spmd(nc, in_maps, core_ids=list(range(B)))
    out = np.stack([np.ascontiguousarray(r["yT"].T) for r in res.results], 0)
    return out.astype(np.float32)
```

```python
import contextlib
import math
import os
DBG = int(os.environ.get('DBG', '99'))
CONV_ENG = os.environ.get('CONV_ENG', 'dve')
import numpy as np
import concourse.bass as bass
import concourse.mybir as mybir
from concourse.bass_utils import run_bass_kernel_spmd

F32 = mybir.dt.float32
BF16 = mybir.dt.bfloat16
AF = mybir.ActivationFunctionType
ALU = mybir.AluOpType

D = 2048
NC = 16
DFF = 5632
NF = 44
DIN = 6672
DEPTH = 4
SEQ = 4096
NHH = 4
NGH = 8
EPS = 1e-6
SEM_LIM = 30000
NEG = -30000.0


class Op:
    __slots__ = ("eng", "fn", "waits", "inc", "dma_key", "seq")

    def __init__(self, eng, fn, dma_key):
        self.eng = eng
        self.fn = fn
        self.waits = []
        self.inc = False
        self.dma_key = dma_key
        self.seq = 0


class Sched:
    ENGS = ("pe", "act", "dve", "pool", "sp")

    def __init__(self):
        self.ops = {e: [] for e in self.ENGS}
        self.last_w = {}
        self.readers = {}
        self.clock = {e: {} for e in self.ENGS}
        self.snap = {}
        self.dma_count = {}
        self.nops = 0

    def add(self, eng, fn, reads=(), writes=(), dma_key=None):
        op = Op(eng, fn, dma_key)
        is_dma = dma_key is not None
        if is_dma:
            n = self.dma_count.get(dma_key, 0) + 1
            self.dma_count[dma_key] = n
            mysrc = "dma:" + dma_key
            myseq = n
        else:
            mysrc = eng
            myseq = len(self.ops[eng]) + 1
        op.seq = len(self.ops[eng]) + 1
        deps = {}
        last_w = self.last_w
        readers = self.readers
        for k in reads:
            lw = last_w.get(k)
            if lw is not None:
                if deps.get(lw[0], 0) < lw[1]:
                    deps[lw[0]] = lw[1]
        for k in writes:
            lw = last_w.get(k)
            if lw is not None and (is_dma or lw[0] != mysrc):
                if deps.get(lw[0], 0) < lw[1]:
                    deps[lw[0]] = lw[1]
            rd = readers.get(k)
            if rd:
                for src, sq in rd.items():
                    if is_dma or src != mysrc:
                        if deps.get(src, 0) < sq:
                            deps[src] = sq
        clk = self.clock[eng]
        for src, sq in deps.items():
            if clk.get(src, 0) >= sq:
                continue
            op.waits.append((src, sq))
            if not src.startswith("dma:"):
                self.ops[src][sq - 1].inc = True
            sn = self.snap.get((src, sq))
            if sn:
                for a, b in sn.items():
                    if clk.get(a, 0) < b:
                        clk[a] = b
            clk[src] = sq
        self.ops[eng].append(op)
        self.snap[(mysrc, myseq)] = dict(clk)
        for k in reads:
            rd = readers.get(k)
            if rd is None:
                readers[k] = {mysrc: myseq}
            else:
                rd[mysrc] = myseq
        for k in writes:
            last_w[k] = (mysrc, myseq)
            readers[k] = {}
        self.nops += 1
        return op

    def emit(self, nc, final_waits):
        inc_prefix = {}
        n_sems = {}
        for e in self.ENGS:
            cnt = 0
            pref = []
            for op in self.ops[e]:
                if op.inc and op.dma_key is None:
                    cnt += 1
                pref.append(cnt)
            inc_prefix[e] = pref
            n_sems[e] = (cnt + SEM_LIM - 1) // SEM_LIM
        dma_lim = SEM_LIM // 16
        with contextlib.ExitStack() as st:
            sems = {}
            for e in self.ENGS:
                sems[e] = [st.enter_context(nc.semaphore(f"s_{e}_{i}")) for i in range(n_sems[e])]
            for k, n in self.dma_count.items():
                ns = (n + dma_lim - 1) // dma_lim
                sems["dma:" + k] = [st.enter_context(nc.semaphore(f"d_{k}_{i}")) for i in range(ns)]

            def wait_target(src, sq):
                if src.startswith("dma:"):
                    ep = (sq - 1) // dma_lim
                    return sems[src][ep], ((sq - 1) % dma_lim + 1) * 16
                c = inc_prefix[src][sq - 1]
                ep = (c - 1) // SEM_LIM
                return sems[src][ep], (c - 1) % SEM_LIM + 1

            def run(e, eng):
                cnt = 0
                for op in self.ops[e]:
                    for src, sq in op.waits:
                        sem, val = wait_target(src, sq)
                        eng.wait_ge(sem, val)
                    ins = op.fn(eng)
                    if op.dma_key is not None:
                        n = op_dma_n[id(op)]
                        ins.then_inc(sems["dma:" + op.dma_key][(n - 1) // dma_lim], 16)
                    elif op.inc:
                        ins.then_inc(sems[e][cnt // SEM_LIM], 1)
                        cnt += 1
                for src, sq in final_waits.get(e, ()):
                    sem, val = wait_target(src, sq)
                    eng.wait_ge(sem, val)

            op_dma_n = {}
            cnts = {}
            for e in self.ENGS:
                for op in self.ops[e]:
                    if op.dma_key is not None:
                        cnts[op.dma_key] = cnts.get(op.dma_key, 0) + 1
                        op_dma_n[id(op)] = cnts[op.dma_key]

            with nc.Block() as block:
                @block.tensor
                def _(eng):
                    run("pe", eng)

                @block.scalar
                def _(eng):
                    run("act", eng)

                @block.vector
                def _(eng):
                    run("dve", eng)

                @block.gpsimd
                def _(eng):
                    run("pool", eng)

                @block.sync
                def _(eng):
                    run("sp", eng)


WIN_SLABS = []
for _g in range(0, 4, 2):
    WIN_SLABS += [6160 + 128 * _g, 6160 + 128 * (_g + 1)]
for _h in range(NHH):
    WIN_SLABS += [0 + 128 * _h, 512 + 128 * _h, 1024 + 128 * _h, 1536 + 128 * _h]
for _h in range(NGH):
    WIN_SLABS += [2048 + 128 * _h, 3072 + 128 * _h, 4096 + 128 * _h, 5120 + 128 * _h]
NSLAB = len(WIN_SLABS)
LBL = 4

def _small_map(L):
    m = {}
    o = 0
    for name, n in (("lbl", 4 * LBL), ("hnw", L), ("gnw", L), ("psc", L * 4), ("convw", L * 24 * 4),
                    ("alog", L * 32), ("dtb", L * 32)):
        m[name] = o
        o += n
    m["_n"] = o
    return m

CON = {"ident": 0, "tri": 128, "triS": 256, "maskL": 384, "maskU": 512, "m01U": 640, "ones": 768, "invc": 896, "rmA": 960, "rmB": 961, "ntri": 964, "_n": 1092}


class _Probe:
    def __init__(self):
        self.rec = None

    def __getattr__(self, name):
        def f(*args, **kw):
            self.rec = (name, args, kw)
            return None
        return f


def _free(ap):
    n = 1
    for d in ap.shape[1:]:
        n *= d
    return n


def op_cost(eng, fn, dma_key):
    pr = _Probe()
    try:
        fn(pr)
    except Exception:
        return 0.3, 0.2
    name, args, kw = pr.rec
    out = kw.get("out", args[0] if args else None)
    if dma_key is not None:
        nbytes = 128 * _free(out) * (4 if out.dtype == F32 else 2)
        return 0.4, 2.0 + nbytes / 150e3
    n = _free(out) if out is not None else 64
    if eng == "pe":
        if name == "transpose":
            return 0.28, 0.25
        lhsT = kw.get("lhsT")
        c = max(n, 64) / 2200.0 + 0.02
        if lhsT is not None and lhsT.dtype == F32:
            c *= 4
        return c, 0.25
    if eng == "act":
        return 0.3 + n / 900.0, 0.2
    if eng == "pool":
        return 0.15 + n / 600.0, 0.25
    return 0.1 + n / 640.0, 0.2


_TAGS = {}
_DELTA = float(os.environ.get('SDELTA', '0'))
_NOWAR = bool(os.environ.get('NOWAR'))
_NOWAR2 = os.environ.get('NOWAR') == '2'


def list_schedule(items):
    n = len(items)
    last_w, readers = {}, {}
    preds = [None] * n
    succs = [[] for _ in range(n)]
    for i, it in enumerate(items):
        p = set()
        for k in it[2]:
            j = last_w.get(k)
            if j is not None:
                p.add(j)
        for k in it[3]:
            j = last_w.get(k)
            if j is not None and not (_NOWAR and (_NOWAR2 or not (isinstance(k, tuple) and k[0] == "ps"))):
                p.add(j)
            rd = readers.get(k)
            if rd and not (_NOWAR and (_NOWAR2 or not (isinstance(k, tuple) and k[0] == "ps"))):
                p.update(rd)
        p.discard(i)
        preds[i] = p
        for j in p:
            succs[j].append(i)
        for k in it[2]:
            readers.setdefault(k, []).append(i)
        for k in it[3]:
            last_w[k] = i
            readers[k] = []
    cost = [(it[5], 0.2) if len(it) > 5 else op_cost(it[0], it[1], it[4]) for it in items]
    prio = [0.0] * n
    for i in range(n - 1, -1, -1):
        m = 0.0
        for j in succs[i]:
            if prio[j] > m:
                m = prio[j]
        prio[i] = cost[i][0] + cost[i][1] + m
    indeg = [len(p) for p in preds]
    ready_t = [0.0] * n
    finish = [0.0] * n
    cand = {}
    for i in range(n):
        if indeg[i] == 0:
            cand.setdefault(items[i][0], []).append(i)
    eng_free = {}
    order = []
    _dbg_starts = {}
    while len(order) < n:
        best = None
        for e, lst in cand.items():
            if not lst:
                continue
            tf = eng_free.get(e, 0.0)
            for i in lst:
                st = ready_t[i] if ready_t[i] > tf else tf
                key = (st, -prio[i], i)
                if best is None or key < best[0]:
                    best = (key, e, i)
        if _DELTA > 0:
            t0 = best[0][0]
            b2 = None
            for e, lst in cand.items():
                tf = eng_free.get(e, 0.0)
                for i in lst:
                    st = ready_t[i] if ready_t[i] > tf else tf
                    if st <= t0 + _DELTA:
                        key = (-prio[i], st, i)
                        if b2 is None or key < b2[0]:
                            b2 = (key, e, i, st)
            best = ((b2[3], 0, 0), b2[1], b2[2])
        (st, _, _), e, i = best
        cand[e].remove(i)
        order.append(i)
        _dbg_starts[i] = st
        eng_free[e] = st + cost[i][0]
        finish[i] = st + cost[i][0] + cost[i][1]
        for j in succs[i]:
            if finish[i] > ready_t[j]:
                ready_t[j] = finish[i]
            indeg[j] -= 1
            if indeg[j] == 0:
                cand.setdefault(items[j][0], []).append(j)
    if os.environ.get("SCHED_DBG") == "2":
        st_of = {}
        t_eng = {}
        lastend = 0.0
        rows = []
        for i in order:
            pass
        starts = _dbg_starts
        pe_ops = [i for i in order if items[i][0] == "pe"]
        _gapsum = {}
        prev_end = 0.0
        for i in pe_ops:
            st = starts[i]
            if st - prev_end > 0.2:
                lim0 = max(preds[i], key=lambda j: finish[j]) if preds[i] else None
                kk = "none" if lim0 is None else (items[lim0][0] + ("/ps" if any(isinstance(k, tuple) and k[0] == "ps" for k in items[lim0][2]) else ""))
                _gapsum[kk] = _gapsum.get(kk, 0.0) + st - prev_end
            if st - prev_end > 400.0:
                lim = max(preds[i], key=lambda j: finish[j]) if preds[i] else None
                def desc(j):
                    pr = _Probe()
                    try:
                        items[j][1](pr)
                        nm = pr.rec[0]
                    except Exception:
                        nm = "?"
                    return "%s:%s w=%s" % (items[j][0], nm, items[j][3][:2])
                print("  PE gap %.1f us at t=%.1f before %s ; limited by %s (fin %.1f)" % (
                    st - prev_end, st, desc(i), desc(lim) if lim is not None else None, finish[lim] if lim is not None else 0))
            prev_end = st + cost[i][0]
    if os.environ.get("SCHED_DBG") == "2":
        tg = {}
        for i in range(n):
            t = _TAGS.get(id(items[i]), "")
            a = tg.get(t)
            st_i = _dbg_starts[i]
            if a is None:
                tg[t] = [st_i, finish[i]]
            else:
                a[0] = min(a[0], st_i); a[1] = max(a[1], finish[i])
        print("  stage spans:", {k: (round(v[0]), round(v[1])) for k, v in tg.items()})
    if os.environ.get("SCHED_DBG") == "2":
        print('  PE gap time by limiting pred engine:', {k: round(v, 1) for k, v in _gapsum.items()})
    if os.environ.get("SCHED_DBG"):
        busy = {}
        for i in range(n):
            busy[items[i][0]] = busy.get(items[i][0], 0.0) + cost[i][0]
        print("SCHED makespan %.1f us busy %s n=%d" % (max(finish), {k: round(v, 1) for k, v in busy.items()}, n), flush=True)
    return order


class Tile:
    def __init__(self, ap, keys, gran=None, off=0, esz=0):
        self.ap = ap
        self.keys = keys
        self.off = off
        self.esz = esz

    def kr(self, c0, c1):
        if self.esz == 0:
            return self.keys
        g0 = (self.off + c0 * self.esz) // 1024
        g1 = (self.off + c1 * self.esz - 1) // 1024
        return [("A", g) for g in range(g0, g1 + 1)]


class Builder:
    def __init__(self, T=SEQ, depth=DEPTH, TT=512, parts=("ffn1", "mix", "ffn2"), final_norm=True,
                 mix_parts=("pool", "hgrn", "gdn")):
        self.T = T
        self.depth = depth
        self.TT = TT
        self.NTG = TT // 512
        self.FG = 22 // self.NTG
        self.NFG = NF // self.FG
        self.parts = parts
        self.mix_parts = mix_parts
        self.final_norm = final_norm
        self.s = Sched()
        self.nc = bass.Bass("TRN2", target_bir_lowering=False)
        self.st = contextlib.ExitStack()
        self.ARENA_BYTES = 80 * 1024
        self.defer = None

    def sb(self, name, shape, dt):
        return self.st.enter_context(self.nc.sbuf_tensor(name, list(shape), dt))

    def dram_in(self, name, shape, dt=F32):
        return self.nc.dram_tensor(name, list(shape), dt, kind="ExternalInput").ap()

    def keys(self, items):
        out = []
        for it in items:
            if isinstance(it, Tile):
                out.extend(it.keys)
            elif isinstance(it, list):
                out.extend(it)
            else:
                out.append(it)
        return out

    def op(self, eng, fn, r=(), w=(), dma_key=None):
        item = (eng, fn, self.keys(r), self.keys(w), dma_key)
        if self.defer is not None:
            self.defer.append(item)
            _TAGS[id(item)] = getattr(self, "tag", "")
            return None
        return self.s.add(*item)

    def wload(self, dst_flat, src32, scr, skey, slotkey, kname, first):
        if not self.use_scr:
            self.op("pool", lambda e: e.dma_start(out=dst_flat, in_=src32), w=[slotkey], dma_key=kname)
        elif first:
            self.op("pool", lambda e: e.dma_start(out=dst_flat, in_=src32), w=[slotkey], dma_key=kname)
            self.op("sp", lambda e: e.dma_start(out=scr, in_=dst_flat), r=[slotkey], w=[skey], dma_key="s" + kname)
        else:
            self.op("sp", lambda e: e.dma_start(out=dst_flat, in_=scr), r=[skey], w=[slotkey], dma_key="h" + kname)

    def areset(self):
        self.aoff = 0
        self.anames = {}

    def A(self, name, cols, dt, like=None):
        esz = 4 if dt == F32 else 2
        nbytes = cols * esz
        if like is not None:
            off, nb = self.anames[like]
            assert nbytes <= nb, (name, like)
        else:
            off = self.aoff
            self.aoff += (nbytes + 1023) // 1024 * 1024
            assert self.aoff <= self.ARENA_BYTES, (name, self.aoff)
        self.anames[name] = (off, (nbytes + 1023) // 1024 * 1024)
        ap = self.arena[:, off // 2: off // 2 + nbytes // 2]
        if dt == F32:
            ap = ap.bitcast(F32)
        keys = [("A", g) for g in range(off // 1024, (off + nbytes - 1) // 1024 + 1)]
        return Tile(ap, keys, off=off, esz=esz)

    def build(self):
        nc, s, op = self.nc, self.s, self.op
        T, TT, L = self.T, self.TT, self.depth
        NTG, FG, NFG = self.NTG, self.FG, self.NFG
        SM = _small_map(L)
        self.SM = SM
        with self.st:
            xT = self.dram_in("xT", [D, T])
            yT = nc.dram_tensor("yT", [D, T], F32, kind="ExternalOutput").ap()
            wgu_d = self.dram_in("wgu", [L, 2, NF, 128, 2 * NC * 128])
            wd_d = self.dram_in("wd", [L, 2, NC, NFG, 128, FG * 128])
            norms_d = self.dram_in("norms", [128, (3 * L + 1) * NC])
            self.win_d = self.dram_in("win", [L, NSLAB // 2, 128, 2 * NC * 128])
            self.wout_d = self.dram_in("wout", [L, NC, 128, NC * 128])
            bgw_d = self.dram_in("bgw", [128, L * NC * 16])
            poolw_d = self.dram_in("poolw", [128, L * 4 * 128])
            small_d = self.dram_in("small", [128, SM["_n"]])
            consts_d = self.dram_in("consts", [128, CON["_n"]])
            self.use_scr = (T // TT) > 1 and not os.environ.get("NOSCR")
            if self.use_scr:
                wgu_s = {(l_, w_): nc.dram_tensor(f"wgu_s{l_}_{w_}", [NF, 128, 2 * NC * 128], BF16).ap()
                         for l_ in range(L) for w_ in range(2)}
                wd_s = {(l_, w_): nc.dram_tensor(f"wd_s{l_}_{w_}", [NC, NFG, 128, FG * 128], BF16).ap()
                        for l_ in range(L) for w_ in range(2)}
                self.win_s = {l_: nc.dram_tensor(f"win_s{l_}", [NSLAB // 2, 128, 2 * NC * 128], BF16).ap()
                              for l_ in range(L)}
                self.wout_s = {l_: nc.dram_tensor(f"wout_s{l_}", [NC, 128, NC * 128], BF16).ap() for l_ in range(L)}
            xTv = xT.rearrange("(c p) t -> p c t", p=128)
            yTv = yT.rearrange("(c p) t -> p c t", p=128)

            x_sb = self.sb("x_sb", [128, NC, TT], F32)
            h_sb = self.sb("h_sb", [128, NC, TT], BF16)
            NWGU, NWD = int(os.environ.get('NWGU', '3')), 2
            wgu_sb = [self.sb(f"wgu_sb{i}", [128, 2, NC, 128], BF16) for i in range(NWGU)]
            wd_sb = [self.sb(f"wd_sb{i}", [128, FG, 128], BF16) for i in range(NWD)]
            self.arena = self.sb("arena", [128, self.ARENA_BYTES // 2], BF16)
            arena = self.arena
            sq_sb = [self.sb(f"sq_sb{i}", [128, 512], BF16) for i in range(2)]
            def top_view(off, cols):
                ap = arena[:, off // 2: off // 2 + cols * 2].bitcast(F32)
                return ap, [("A", g) for g in range(off // 1024, (off + cols * 4 - 1) // 1024 + 1)]
            AB = self.ARENA_BYTES
            sg_sb, K_sg = [], []
            for i in range(2):
                ap_, k_ = top_view(AB - TT * 4 - (i + 1) * 2048, 512)
                sg_sb.append(ap_)
                K_sg.append(k_)
            rstd_sb, K_rstd_all = top_view(AB - TT * 4, TT)
            K_rstd = [[("A", (AB - TT * 4 + tg * 2048) // 1024), ("A", (AB - TT * 4 + tg * 2048) // 1024 + 1)]
                      for tg in range(NTG)]
            norms_sb = self.sb("norms_sb", [128, (3 * L + 1) * NC], F32)
            ones_bf = self.sb("ones_bf", [128, 128], BF16)
            epsD_sb = self.sb("epsD_sb", [128, 4], F32)
            small_sb = self.sb("small_sb", [128, SM["_n"]], F32)
            con_sb = self.sb("con_sb", [128, CON["_n"]], F32)
            ident_bf = self.sb("ident_bf", [128, 128], BF16)
            bgw_sb = self.sb("bgw_sb", [128, L, NC, 16], BF16)
            poolw_sb = self.sb("poolw_sb", [128, L, 4, 128], BF16)
            lb_sb = self.sb("lb_sb", [128, 4, LBL], F32)
            oml_sb = self.sb("oml_sb", [128, 4, LBL], F32)
            lbe_sb = self.sb("lbe_sb", [128, 4, LBL], F32)
            lbs_sb = self.sb("lbs_sb", [128, 4], F32)
            nexpA_sb = self.sb("nexpA_sb", [128, L * 32], F32)
            Sh_sb = self.sb("Sh_sb", [128, L, NHH, 128], F32)
            Sg_sb = self.sb("Sg_sb", [128, L, NGH, 128], F32)
            ctail_sb = self.sb("ctail_sb", [128, L, 24, 3], F32)
            ptail_sb = self.sb("ptail_sb", [128, L, 4, 16], F32)
            bg_sb = self.sb("bg_sb", [128, 12, 32], F32)
            ps = [self.st.enter_context(nc.psum_tensor(f"ps{i}", [128, 512], F32)) for i in range(8)]
            self.__dict__.update(dict(x_sb=x_sb, h_sb=h_sb, wgu_sb=wgu_sb, wd_sb=wd_sb, sq_sb=sq_sb, sg_sb=sg_sb,
                                      rstd_sb=rstd_sb, ones_bf=ones_bf, epsD_sb=epsD_sb, small_sb=small_sb,
                                      con_sb=con_sb, ident_bf=ident_bf, bgw_sb=bgw_sb, poolw_sb=poolw_sb,
                                      lb_sb=lb_sb, oml_sb=oml_sb, nexpA_sb=nexpA_sb, Sh_sb=Sh_sb, Sg_sb=Sg_sb,
                                      ctail_sb=ctail_sb, ptail_sb=ptail_sb, bg_sb=bg_sb, ps=ps, NWGU=NWGU, NWD=NWD))
            self.wgu_i = 0
            self.wd_i = 0

            op("sp", lambda e: e.dma_start(out=norms_sb[:, :], in_=norms_d[:, :]), w=["norms"], dma_key="norms")
            op("sp", lambda e: e.dma_start(out=small_sb[:, :], in_=small_d[:, :]), w=["small"], dma_key="small")
            op("sp", lambda e: e.dma_start(out=con_sb[:, :], in_=consts_d[:, :]), w=["con"], dma_key="con")
            op("pool", lambda e: e.dma_start(out=bgw_sb[:, :, :, :].rearrange("p l k c -> p (l k c)"), in_=bgw_d[:, :]),
               w=["bgw"], dma_key="bgw")
            op("pool", lambda e: e.dma_start(out=poolw_sb[:, :, :, :].rearrange("p l g d -> p (l g d)"),
                                             in_=poolw_d[:, :]), w=["poolw"], dma_key="poolw")
            op("dve", lambda e: e.memset(ones_bf[:, :], 1.0), w=["ones"])
            op("dve", lambda e: e.memset(epsD_sb[:, 0:1], D * EPS), w=["epsD"])
            op("dve", lambda e: e.memset(epsD_sb[:, 1:2], 128 * EPS), w=["epsD"])
            op("dve", lambda e: e.memset(epsD_sb[:, 2:3], EPS), w=["epsD"])
            op("dve", lambda e: e.memset(epsD_sb[:, 3:4], 1.0), w=["epsD"])
            op("dve", lambda e: e.tensor_scalar(out=norms_sb[:, :], in0=norms_sb[:, :], scalar1=math.sqrt(D),
                                                scalar2=None, op0=ALU.mult), r=["norms"], w=["norms"])
            op("dve", lambda e: e.tensor_copy(out=ident_bf[:, :], in_=con_sb[:, CON["ident"]:CON["ident"] + 128]),
               r=["con"], w=["identbf"])
            for nm in ("hnw", "gnw"):
                o0 = SM[nm]
                op("dve", lambda e, o0=o0: e.tensor_scalar(out=small_sb[:, o0:o0 + L], in0=small_sb[:, o0:o0 + L],
                                                          scalar1=math.sqrt(128.0), scalar2=None, op0=ALU.mult),
                   r=["small"], w=["small"])
            op("dve", lambda e: e.memset(Sh_sb[:, :, :, :].rearrange("p l h d -> p (l h d)"), 0.0),
               w=["Sh%d" % i for i in range(NHH)])
            op("dve", lambda e: e.memset(Sg_sb[:, :, :, :].rearrange("p l h d -> p (l h d)"), 0.0),
               w=["Sg%d" % i for i in range(NGH)])
            op("dve", lambda e: e.memset(ctail_sb[:, :, :, :].rearrange("p l c t -> p (l c t)"), 0.0), w=["ctail"])
            op("dve", lambda e: e.memset(ptail_sb[:, :, :, :].rearrange("p l c t -> p (l c t)"), 0.0), w=["ptail"])
            lblv = small_sb[:, SM["lbl"]:SM["lbl"] + 4 * LBL].rearrange("p (h l) -> p h l", l=LBL)
            op("act", lambda e: e.activation(out=lbe_sb[:, :, :], in_=lblv, func=AF.Exp), r=["small"], w=["lbe"])
            op("dve", lambda e: e.tensor_reduce(out=lbs_sb[:, :], in_=lbe_sb[:, :, :], axis=mybir.AxisListType.X,
                                                op=ALU.add), r=["lbe"], w=["lbs"])
            op("dve", lambda e: e.reciprocal(out=lbs_sb[:, :], in_=lbs_sb[:, :]), r=["lbs"], w=["lbs"])
            op("dve", lambda e: e.memset(lb_sb[:, :, 0], 0.0), w=["lb"])
            for li in range(1, LBL):
                op("dve", lambda e, li=li: e.tensor_tensor(out=lbe_sb[:, :, li], in0=lbe_sb[:, :, li], in1=lbs_sb[:, :],
                                                           op=ALU.mult), r=["lbe", "lbs"], w=["lbe"])
                op("dve", lambda e, li=li: e.tensor_tensor(out=lb_sb[:, :, li], in0=lb_sb[:, :, li - 1],
                                                           in1=lbe_sb[:, :, li], op=ALU.add), r=["lbe", "lb"], w=["lb"])
            op("dve", lambda e: e.tensor_scalar(out=oml_sb[:, :, :], in0=lb_sb[:, :, :], scalar1=-1.0, scalar2=1.0,
                                                op0=ALU.mult, op1=ALU.add), r=["lb"], w=["oml"])
            a0 = SM["alog"]
            op("act", lambda e: e.activation(out=nexpA_sb[:, :], in_=small_sb[:, a0:a0 + L * 32], func=AF.Exp),
               r=["small"], w=["nexpA"])
            op("dve", lambda e: e.tensor_scalar(out=nexpA_sb[:, :], in0=nexpA_sb[:, :], scalar1=-1.0, scalar2=None,
                                                op0=ALU.mult), r=["nexpA"], w=["nexpA"])

            def rms_stats():
                for tg in range(NTG):
                    tsl = slice(tg * 512, (tg + 1) * 512)
                    for c in range(NC):
                        sq = sq_sb[c % 2]
                        op("act", lambda e, sq=sq, c=c, tsl=tsl: e.activation(out=sq[:, :], in_=x_sb[:, c, tsl],
                                                                          func=AF.Square),
                           r=[("x", c, tg)], w=[("sq", c % 2)])
                        op("pe", lambda e, sq=sq, c=c: e.matmul(ps[6][:, :], lhsT=ones_bf[:, :], rhs=sq[:, :],
                                                                start=(c == 0), stop=(c == NC - 1)),
                           r=[("sq", c % 2), "ones"], w=[("ps", 6)])
                    op("act", lambda e, tsl=tsl: e.activation(out=rstd_sb[:, tsl], in_=ps[6][:, :], func=AF.Sqrt,
                                                              bias=epsD_sb[:, 0:1], scale=1.0),
                       r=[("ps", 6), "epsD"], w=[K_rstd[tg]])
                    op("dve", lambda e, tsl=tsl: e.reciprocal(out=rstd_sb[:, tsl], in_=rstd_sb[:, tsl]),
                       r=[K_rstd[tg]], w=[K_rstd[tg]])

            def rmsnorm_h(nidx):
                rms_stats()
                for tg in range(NTG):
                    tsl = slice(tg * 512, (tg + 1) * 512)
                    for c in range(NC):
                        col = nidx * NC + c
                        op("dve", lambda e, c=c, tsl=tsl, col=col: e.scalar_tensor_tensor(
                            out=h_sb[:, c, tsl], in0=x_sb[:, c, tsl], scalar=norms_sb[:, col:col + 1],
                            in1=rstd_sb[:, tsl], op0=ALU.mult, op1=ALU.mult),
                            r=[("x", c, tg), K_rstd[tg], "norms"], w=[("h", c, tg)])
            self.rmsnorm_h = rmsnorm_h

            act_v = arena[:, 0:FG * TT].rearrange("p (f t) -> p f t", t=TT)

            def akey(fi, tg):
                return ("A", (fi * TT + tg * 512) * 2 // 1024)

            def ffn(l, which):
                rmsnorm_h(3 * l + (0 if which == 0 else 2))
                for fg in range(NFG):
                    for fi in range(FG):
                        fc = fg * FG + fi
                        slot = self.wgu_i % NWGU
                        self.wgu_i += 1
                        wt = wgu_sb[slot]
                        self.wload(wt[:, :, :, :].rearrange("p a k f -> p (a k f)"), wgu_d[l, which, fc, :, :],
                                   wgu_s[(l, which)][fc, :, :] if self.use_scr else None, ("scr_gu", l, which, fc),
                                   ("wgu", slot), f"wgu{slot}", self.cur_ti == 0)
                        for tg in range(NTG):
                            tsl = slice(tg * 512, (tg + 1) * 512)
                            par = (fc * NTG + tg) % 2
                            pg, pu = ps[2 * par], ps[2 * par + 1]
                            for gu, pp in ((0, pg), (1, pu)):
                                for kc in range(NC):
                                    op("pe", lambda e, wt=wt, gu=gu, kc=kc, pp=pp, tsl=tsl: e.matmul(
                                        pp[:, :], lhsT=wt[:, gu, kc, :], rhs=h_sb[:, kc, tsl],
                                        start=(kc == 0), stop=(kc == NC - 1)),
                                        r=[("wgu", slot), ("h", kc, tg)], w=[("ps", 2 * par + gu)])
                            sg = sg_sb[par]
                            op("act", lambda e, sg=sg, pg=pg: e.activation(out=sg[:, :], in_=pg[:, :], func=AF.Silu),
                               r=[("ps", 2 * par)], w=[K_sg[par]])
                            op("dve", lambda e, sg=sg, pu=pu, fi=fi, tsl=tsl: e.tensor_tensor(
                                out=act_v[:, fi, tsl], in0=sg[:, :], in1=pu[:, :], op=ALU.mult),
                                r=[K_sg[par], ("ps", 2 * par + 1)], w=[akey(fi, tg)])
                    for dc in range(NC):
                        slot = self.wd_i % NWD
                        self.wd_i += 1
                        wt = wd_sb[slot]
                        self.wload(wt[:, :, :].rearrange("p f d -> p (f d)"), wd_d[l, which, dc, fg, :, :],
                                   wd_s[(l, which)][dc, fg, :, :] if self.use_scr else None, ("scr_d", l, which, dc, fg),
                                   ("wd", slot), f"wd{slot}", self.cur_ti == 0)
                        for tg in range(NTG):
                            tsl = slice(tg * 512, (tg + 1) * 512)
                            par = (dc * NTG + tg) % 2
                            po = ps[4 + par]
                            for fi in range(FG):
                                op("pe", lambda e, wt=wt, fi=fi, po=po, tsl=tsl: e.matmul(
                                    po[:, :], lhsT=wt[:, fi, :], rhs=act_v[:, fi, tsl],
                                    start=(fi == 0), stop=(fi == FG - 1)),
                                    r=[("wd", slot), akey(fi, tg)], w=[("ps", 4 + par)])
                            op("dve", lambda e, po=po, dc=dc, tsl=tsl: e.scalar_tensor_tensor(
                                out=x_sb[:, dc, tsl], in0=po[:, :], scalar=0.5, in1=x_sb[:, dc, tsl],
                                op0=ALU.mult, op1=ALU.add),
                                r=[("ps", 4 + par), ("x", dc, tg)], w=[("x", dc, tg)])

            ntiles = T // TT
            for ti in range(ntiles):
                t0 = ti * TT
                self.cur_ti = ti
                op("sp", lambda e, t0=t0: e.dma_start(out=x_sb[:, :, :], in_=xTv[:, :, t0:t0 + TT]),
                   w=[("x", c, tg) for c in range(NC) for tg in range(NTG)], dma_key="xin")
                for l in range(L):
                    if "ffn1" in self.parts:
                        ffn(l, 0)
                    if "mix" in self.parts:
                        for tg in range(NTG):
                            self.mixer(l, ti * NTG + tg, tg)
                    if "ffn2" in self.parts:
                        ffn(l, 1)
                if self.final_norm:
                    rms_stats()
                    for tg in range(NTG):
                        tsl = slice(tg * 512, (tg + 1) * 512)
                        for c in range(NC):
                            col = 3 * L * NC + c
                            op("dve", lambda e, c=c, tsl=tsl, col=col: e.scalar_tensor_tensor(
                                out=x_sb[:, c, tsl], in0=x_sb[:, c, tsl], scalar=norms_sb[:, col:col + 1],
                                in1=rstd_sb[:, tsl], op0=ALU.mult, op1=ALU.mult),
                                r=[("x", c, tg), K_rstd[tg], "norms"], w=[("x", c, tg)])
                op("sp", lambda e, t0=t0: e.dma_start(out=yTv[:, :, t0:t0 + TT], in_=x_sb[:, :, :]),
                   r=[("x", c, tg) for c in range(NC) for tg in range(NTG)], dma_key="yout")

            nout = s.dma_count["yout"]
            s.emit(nc, {"sp": [("dma:yout", nout)]})
        return nc

    def win_pair(self, l, pair):
        slot = self.wgu_i % self.NWGU
        self.wgu_i += 1
        wt = self.wgu_sb[slot]
        win_d = self.win_d
        self.wload(wt[:, :, :, :].rearrange("p a k f -> p (a k f)"), win_d[l, pair, :, :],
                   self.win_s[l][pair, :, :] if self.use_scr else None, ("scr_in", l, pair),
                   ("wgu", slot), f"wgu{slot}", self.cur_ti == 0)
        return wt, ("wgu", slot)

    def proj_fm(self, wt, wkey, a, bank, tg):
        ps, h_sb = self.ps, self.h_sb
        tsl = slice(tg * 512, (tg + 1) * 512)
        for kc in range(NC):
            self.op("pe", lambda e, kc=kc: e.matmul(ps[bank][:, :], lhsT=wt[:, a, kc, :], rhs=h_sb[:, kc, tsl],
                                                   start=(kc == 0), stop=(kc == NC - 1)),
                    r=[wkey, ("h", kc, tg)], w=[("ps", bank)])

    def mixer(self, l, gti, tg):
        op, ps, A = self.op, self.ps, self.A
        L = self.depth
        SM = self.SM
        small, con = self.small_sb, self.con_sb
        h_sb, x_sb = self.h_sb, self.x_sb
        tsl = slice(tg * 512, (tg + 1) * 512)
        ones_bf, ident_bf, epsD = self.ones_bf, self.ident_bf, self.epsD_sb
        sq_sb = self.sq_sb
        c_ident = con[:, CON["ident"]:CON["ident"] + 128]
        c_tri = con[:, CON["tri"]:CON["tri"] + 128]
        c_triS = con[:, CON["triS"]:CON["triS"] + 128]
        c_maskL = con[:, CON["maskL"]:CON["maskL"] + 128]
        c_maskU = con[:, CON["maskU"]:CON["maskU"] + 128]
        c_m01U = con[:, CON["m01U"]:CON["m01U"] + 128]
        c_ones = con[:, CON["ones"]:CON["ones"] + 128]
        psT = ps[7][:, :].bitcast(BF16)

        if tg == 0:
            self.rmsnorm_h(3 * l + 1)
        self.areset()
        outer_defer = []
        self.defer = outer_defer
        self.outer_defer = outer_defer
        y = A("y", NC * 512, BF16)
        yv = y.ap.rearrange("p (c t) -> p c t", t=512)

        def ykey(c):
            return y.kr(c * 512, (c + 1) * 512)

        def head_out(po_bank, gate_t, nw_col, ychunk, stat_bank=6):
            sq = sq_sb[0]
            op("act", lambda e: e.activation(out=sq[:, :], in_=ps[po_bank][:, :], func=AF.Square),
               r=[("ps", po_bank)], w=[("sq", 0)])
            op("pe", lambda e: e.matmul(ps[stat_bank][:, :], lhsT=ones_bf[:, :], rhs=sq[:, :], start=True, stop=True),
               r=[("sq", 0), "ones"], w=[("ps", stat_bank)])
            rs = self._rs
            op("act", lambda e: e.activation(out=rs.ap, in_=ps[stat_bank][:, :], func=AF.Sqrt, bias=epsD[:, 1:2], scale=1.0),
               r=[("ps", stat_bank), "epsD"], w=[rs])
            op("dve", lambda e: e.reciprocal(out=rs.ap, in_=rs.ap), r=[rs], w=[rs])
            t1 = self._t1
            op("dve", lambda e: e.scalar_tensor_tensor(out=t1.ap, in0=ps[po_bank][:, :],
                                                       scalar=small[:, nw_col:nw_col + 1], in1=rs.ap,
                                                       op0=ALU.mult, op1=ALU.mult),
               r=[("ps", po_bank), "small", rs], w=[t1])
            op("dve", lambda e: e.tensor_tensor(out=yv[:, ychunk, :], in0=t1.ap, in1=gate_t.ap, op=ALU.mult),
               r=[t1, gate_t], w=[ykey(ychunk)])

        self._rs = A("rs", 512, F32)
        self._t1 = A("t1", 512, F32)
        amark = self.aoff

        bg = self.bg_sb
        if "gdn" in self.mix_parts:
            bgw = self.bgw_sb
            for blk in range(4):
                for kc in range(NC):
                    op("pe", lambda e, blk=blk, kc=kc: e.matmul(
                        ps[4][:, blk * 16:(blk + 1) * 16], lhsT=h_sb[:, kc, tg * 512 + blk * 128: tg * 512 + (blk + 1) * 128],
                        rhs=bgw[:, l, kc, :], start=(kc == 0), stop=(kc == NC - 1)),
                        r=[("h", kc, tg), "bgw"], w=[("ps", 4)])
            pbg = ps[4][:, 0:64].rearrange("p (b c) -> p b c", c=16)
            BETA, NBETA, XA, GG, NGG, EGT, EGL, COEF = range(8)

            def bgv(i):
                return bg[:, i, :].rearrange("p (b h) -> p b h", h=8)
            op("act", lambda e: e.activation(out=bgv(BETA), in_=pbg[:, :, 0:8], func=AF.Sigmoid),
               r=[("ps", 4)], w=[("bg", BETA)])
            op("dve", lambda e: e.tensor_scalar(out=bg[:, NBETA, :], in0=bg[:, BETA, :], scalar1=-1.0, scalar2=None,
                                                op0=ALU.mult), r=[("bg", BETA)], w=[("bg", NBETA)])
            d0 = SM["dtb"] + l * 32
            op("dve", lambda e: e.tensor_tensor(out=bgv(XA), in0=pbg[:, :, 8:16],
                                                in1=small[:, d0:d0 + 32].rearrange("p (b h) -> p b h", h=8),
                                                op=ALU.add), r=[("ps", 4), "small"], w=[("bg", XA)])
            op("act", lambda e: e.activation(out=bg[:, XA, :], in_=bg[:, XA, :], func=AF.Exp),
               r=[("bg", XA)], w=[("bg", XA)])
            op("act", lambda e: e.activation(out=bg[:, XA, :], in_=bg[:, XA, :], func=AF.Ln, bias=epsD[:, 3:4], scale=1.0),
               r=[("bg", XA), "epsD"], w=[("bg", XA)])
            nA = self.nexpA_sb
            op("dve", lambda e: e.tensor_tensor(out=bg[:, GG, :], in0=bg[:, XA, :], in1=nA[:, l * 32:(l + 1) * 32],
                                                op=ALU.mult), r=[("bg", XA), "nexpA"], w=[("bg", GG)])
            op("dve", lambda e: e.tensor_scalar(out=bg[:, NGG, :], in0=bg[:, GG, :], scalar1=-1.0, scalar2=None,
                                                op0=ALU.mult), r=[("bg", GG)], w=[("bg", NGG)])
            for blk in range(4):
                op("pe", lambda e, blk=blk: e.matmul(ps[5][:, blk * 8:(blk + 1) * 8], lhsT=c_tri,
                                                     rhs=bg[:, GG, blk * 8:(blk + 1) * 8], start=True, stop=True),
                   r=["con", ("bg", GG)], w=[("ps", 5)])
                op("pe", lambda e, blk=blk: e.matmul(ps[5][:, 32 + blk * 8:32 + (blk + 1) * 8], lhsT=c_triS,
                                                     rhs=bg[:, GG, blk * 8:(blk + 1) * 8], start=True, stop=True),
                   r=["con", ("bg", GG)], w=[("ps", 5)])
            op("act", lambda e: e.activation(out=bg[:, EGT, :], in_=ps[5][:, 0:32], func=AF.Exp),
               r=[("ps", 5)], w=[("bg", EGT)])
            op("act", lambda e: e.activation(out=bg[:, EGL, :], in_=ps[5][:, 32:64], func=AF.Exp),
               r=[("ps", 5)], w=[("bg", EGL)])
            op("dve", lambda e: e.tensor_tensor(out=bg[:, COEF, :], in0=bg[:, BETA, :], in1=bg[:, EGT, :], op=ALU.mult),
               r=[("bg", BETA), ("bg", EGT)], w=[("bg", COEF)])
            op("dve", lambda e: e.tensor_scalar(out=bg[:, 8, :], in0=bg[:, EGL, :], scalar1=con[:, CON["rmB"]:CON["rmB"] + 1],
                                                scalar2=None, op0=ALU.mult), r=[("bg", EGL), "con"], w=[("bg", 8)])
            op("dve", lambda e: e.tensor_scalar(out=bg[:, EGL, :], in0=bg[:, EGL, :], scalar1=con[:, CON["rmA"]:CON["rmA"] + 1],
                                                scalar2=None, op0=ALU.mult), r=[("bg", EGL), "con"], w=[("bg", EGL)])

        if "pool" in self.mix_parts:
            ubuf = A("ubuf", 528, F32)
            sA = A("sA", 528, F32)
            sB = A("sB", 528, F32)
            m_bf = A("m_bf", 512, BF16)
            ptail = self.ptail_sb
            poolw = self.poolw_sb
            for gi in range(4):
                if gi % 2 == 0:
                    wt, wkey = self.win_pair(l, gi // 2)
                self.proj_fm(wt, wkey, gi % 2, 0, tg)
                op("dve", lambda e, gi=gi: e.tensor_copy(out=ubuf.ap[:, 0:16], in_=ptail[:, l, gi, :]),
                   r=["ptail"], w=[ubuf])
                op("act", lambda e: e.activation(out=ubuf.ap[:, 16:528], in_=ps[0][:, :], func=AF.Copy),
                   r=[("ps", 0)], w=[ubuf])
                op("dve", lambda e, gi=gi: e.tensor_copy(out=ptail[:, l, gi, :], in_=ubuf.ap[:, 512:528]),
                   r=[ubuf], w=["ptail"])
                src = ubuf
                sh = 1
                bufs = [sA, sB]
                for j in range(gi + 1):
                    dst = bufs[j % 2]
                    lo = 2 * sh - 1
                    op("dve", lambda e, src=src, dst=dst, lo=lo, sh=sh: e.tensor_tensor(
                        out=dst.ap[:, lo:528], in0=src.ap[:, lo:528], in1=src.ap[:, lo - sh:528 - sh], op=ALU.add),
                        r=[src], w=[dst])
                    src = dst
                    sh *= 2
                win = sh
                op("dve", lambda e, src=src, win=win: e.scalar_tensor_tensor(
                    out=m_bf.ap, in0=src.ap[:, 16:528], scalar=1.0 / win, in1=ubuf.ap[:, 16:528],
                    op0=ALU.mult, op1=ALU.subtract), r=[src, ubuf], w=[m_bf])
                if gti == 0:
                    i0 = CON["invc"] + gi * 16
                    tmpc = self._t1
                    op("dve", lambda e, src=src, i0=i0: e.tensor_tensor(out=tmpc.ap[:, 0:16], in0=src.ap[:, 16:32],
                                                                        in1=con[:, i0:i0 + 16], op=ALU.mult),
                       r=[src, "con"], w=[tmpc])
                    op("dve", lambda e: e.tensor_tensor(out=m_bf.ap[:, 0:16], in0=tmpc.ap[:, 0:16],
                                                        in1=ubuf.ap[:, 16:32], op=ALU.subtract),
                       r=[tmpc, ubuf], w=[m_bf])
                op("pe", lambda e, gi=gi: e.matmul(ps[1][:, :], lhsT=poolw[:, l, gi, :], rhs=m_bf.ap,
                                                   start=True, stop=True), r=["poolw", m_bf], w=[("ps", 1)])
                pc = SM["psc"] + l * 4 + gi
                op("dve", lambda e, gi=gi, pc=pc: e.tensor_scalar(out=yv[:, 12 + gi, :], in0=ps[1][:, :],
                                                                  scalar1=small[:, pc:pc + 1], scalar2=None,
                                                                  op0=ALU.mult),
                   r=[("ps", 1), "small"], w=[ykey(12 + gi)])
        else:
            for gi in range(4):
                op("dve", lambda e, gi=gi: e.memset(yv[:, 12 + gi, :], 0.0), w=[ykey(12 + gi)])

        self.aoff = amark
        if "hgrn" in self.mix_parts:
            Sh = self.Sh_sb
            hsets = []
            for par in range(2):
                hsets.append(dict(
                    q32=A(f"hq32{par}", 512, F32), f32=A(f"hf32{par}", 512, F32), k32=A(f"hk32{par}", 512, F32),
                    cum=A(f"hcum{par}", 512, F32), ecum=A(f"hecum{par}", 512, F32), gsil=A(f"hgsil{par}", 512, F32),
                    qt=A(f"hqt{par}", 512, BF16), kt=A(f"hkt{par}", 512, BF16), kdT=A(f"hkdT{par}", 512, BF16),
                    v_bf=A(f"hv{par}", 512, BF16), kdec=A(f"hkdec{par}", 512, BF16), kdecB=A(f"hkdecB{par}", 512, BF16),
                    scT=A(f"hscT{par}", 512, BF16), Sb32=A(f"hSb32{par}", 9 * 128, F32), Sbf=A(f"hSbf{par}", 8 * 128, BF16)))
            for hh in range(NHH):
                par = hh % 2
                T_ = hsets[par]
                q32, f32, k32, cum, ecum, gsil, qt, kt, kdT, v_bf, kdec, kdecB, scT, Sb32, Sbf = (T_[k] for k in (
                    "q32", "f32", "k32", "cum", "ecum", "gsil", "qt", "kt", "kdT", "v_bf", "kdec", "kdecB", "scT", "Sb32", "Sbf"))
                B0, B1, B2, B3 = (0, 1, 2, 3) if par == 0 else (4, 5, 6, 7)
                psTh = ps[B0][:, :].bitcast(BF16)
                wtA, kA = self.win_pair(l, 2 + 2 * hh)
                wtB, kB = self.win_pair(l, 2 + 2 * hh + 1)
                self.proj_fm(wtA, kA, 0, B0, tg)
                self.proj_fm(wtA, kA, 1, B1, tg)
                self.proj_fm(wtB, kB, 1, B3, tg)
                for blk in range(4):
                    for kc in range(NC):
                        op("pe", lambda e, blk=blk, kc=kc, wtB=wtB, B2=B2: e.matmul(
                            ps[B2][:, blk * 128:(blk + 1) * 128],
                            lhsT=h_sb[:, kc, tg * 512 + blk * 128: tg * 512 + (blk + 1) * 128], rhs=wtB[:, 0, kc, :],
                            start=(kc == 0), stop=(kc == NC - 1)), r=[("h", kc, tg), kB], w=[("ps", B2)])
                op("act", lambda e, q32=q32, B0=B0: e.activation(out=q32.ap, in_=ps[B0][:, :], func=AF.Silu), r=[("ps", B0)], w=[q32])
                op("act", lambda e, f32=f32, B1=B1: e.activation(out=f32.ap, in_=ps[B1][:, :], func=AF.Sigmoid), r=[("ps", B1)], w=[f32])
                op("act", lambda e, gsil=gsil, B3=B3: e.activation(out=gsil.ap, in_=ps[B3][:, :], func=AF.Silu), r=[("ps", B3)], w=[gsil])
                op("act", lambda e, v_bf=v_bf, B2=B2: e.activation(out=v_bf.ap, in_=ps[B2][:, :], func=AF.Copy), r=[("ps", B2)], w=[v_bf])
                op("dve", lambda e, hh=hh, f32=f32: e.tensor_scalar(out=f32.ap, in0=f32.ap, scalar1=self.oml_sb[:, hh, l:l + 1],
                                                                    scalar2=self.lb_sb[:, hh, l:l + 1], op0=ALU.mult, op1=ALU.add),
                   r=[f32, "oml", "lb"], w=[f32])
                op("dve", lambda e, f32=f32, k32=k32: e.tensor_scalar(out=k32.ap, in0=f32.ap, scalar1=-1.0, scalar2=1.0,
                                                                      op0=ALU.mult, op1=ALU.add), r=[f32], w=[k32])
                op("act", lambda e, f32=f32: e.activation(out=f32.ap, in_=f32.ap, func=AF.Ln), r=[f32], w=[f32])
                for c in range(8):
                    op("dve", lambda e, c=c, cum=cum, f32=f32: e.tensor_tensor_scan(
                        out=cum.ap[:, c * 64:(c + 1) * 64], data0=c_ones[:, 0:64], data1=f32.ap[:, c * 64:(c + 1) * 64],
                        initial=0.0, op0=ALU.mult, op1=ALU.add), r=[f32, "con"], w=[cum])
                op("act", lambda e, cum=cum, ecum=ecum: e.activation(out=ecum.ap, in_=cum.ap, func=AF.Exp), r=[cum], w=[ecum])
                op("dve", lambda e, qt=qt, q32=q32, ecum=ecum: e.tensor_tensor(out=qt.ap, in0=q32.ap, in1=ecum.ap, op=ALU.mult),
                   r=[q32, ecum], w=[qt])
                op("act", lambda e, cum=cum: e.activation(out=cum.ap, in_=cum.ap, func=AF.Exp, scale=-1.0), r=[cum], w=[cum])
                op("dve", lambda e, k32=k32, cum=cum: e.tensor_tensor(out=k32.ap, in0=k32.ap, in1=cum.ap, op=ALU.mult),
                   r=[k32, cum], w=[k32])
                op("act", lambda e, kt=kt, k32=k32: e.activation(out=kt.ap, in_=k32.ap, func=AF.Copy), r=[k32], w=[kt])
                for c in range(8):
                    op("dve", lambda e, c=c, kdT=kdT, k32=k32, ecum=ecum: e.tensor_scalar(
                        out=kdT.ap[:, c * 64:(c + 1) * 64], in0=k32.ap[:, c * 64:(c + 1) * 64],
                        scalar1=ecum.ap[:, c * 64 + 63:c * 64 + 64], scalar2=None, op0=ALU.mult), r=[k32, ecum], w=[kdT])
                for blk in range(4):
                    op("pe", lambda e, blk=blk, kdT=kdT, psTh=psTh: e.transpose(
                        out=psTh[:, blk * 128:(blk + 1) * 128], in_=kdT.ap[:, blk * 128:(blk + 1) * 128],
                        identity=ident_bf[:, :]), r=[kdT, "identbf"], w=[("ps", B0)])
                op("dve", lambda e, kdec=kdec, psTh=psTh: e.tensor_scalar(
                    out=kdec.ap, in0=psTh[:, 0:512], scalar1=con[:, CON["rmA"]:CON["rmA"] + 1], scalar2=None, op0=ALU.mult),
                    r=[("ps", B0), "con"], w=[kdec])
                op("dve", lambda e, kdecB=kdecB, psTh=psTh: e.tensor_scalar(
                    out=kdecB.ap, in0=psTh[:, 0:512], scalar1=con[:, CON["rmB"]:CON["rmB"] + 1], scalar2=None, op0=ALU.mult),
                    r=[("ps", B0), "con"], w=[kdecB])
                for blk in range(4):
                    op("pe", lambda e, blk=blk, kt=kt, qt=qt, B1=B1: e.matmul(
                        ps[B1][:, blk * 128:(blk + 1) * 128], lhsT=kt.ap[:, blk * 128:(blk + 1) * 128],
                        rhs=qt.ap[:, blk * 128:(blk + 1) * 128], start=True, stop=True), r=[kt, qt], w=[("ps", B1)])
                op("dve", lambda e, scT=scT, B1=B1: e.tensor_tensor(
                    out=scT.ap.rearrange("p (b t) -> p b t", t=128), in0=ps[B1][:, :].rearrange("p (b t) -> p b t", t=128),
                    in1=c_m01U.unsqueeze(1).to_broadcast([128, 4, 128]), op=ALU.mult), r=[("ps", B1), "con"], w=[scT])
                for c in range(8):
                    b = c // 2
                    bank = B2 if c < 4 else B3
                    kd = kdec if c % 2 == 0 else kdecB
                    op("pe", lambda e, c=c, b=b, bank=bank, kd=kd, v_bf=v_bf: e.matmul(
                        ps[bank][:, (c % 4) * 128:(c % 4 + 1) * 128], lhsT=kd.ap[:, b * 128:(b + 1) * 128],
                        rhs=v_bf.ap[:, b * 128:(b + 1) * 128], start=True, stop=True),
                        r=[kd, v_bf], w=[("ps", bank)])
                op("dve", lambda e, hh=hh, Sb32=Sb32: e.tensor_copy(out=Sb32.ap[:, 0:128], in_=Sh[:, l, hh, :]),
                   r=["Sh%d" % hh], w=[Sb32])
                for c in range(8):
                    bank = B2 if c < 4 else B3
                    op("dve", lambda e, c=c, bank=bank, Sb32=Sb32, ecum=ecum: e.scalar_tensor_tensor(
                        out=Sb32.ap[:, (c + 1) * 128:(c + 2) * 128], in0=Sb32.ap[:, c * 128:(c + 1) * 128],
                        scalar=ecum.ap[:, c * 64 + 63:c * 64 + 64], in1=ps[bank][:, (c % 4) * 128:(c % 4 + 1) * 128],
                        op0=ALU.mult, op1=ALU.add), r=[Sb32, ecum, ("ps", bank)], w=[Sb32])
                op("act", lambda e, Sbf=Sbf, Sb32=Sb32: e.activation(out=Sbf.ap, in_=Sb32.ap[:, 0:1024], func=AF.Copy),
                   r=[Sb32], w=[Sbf])
                op("dve", lambda e, hh=hh, Sb32=Sb32: e.tensor_copy(out=Sh[:, l, hh, :], in_=Sb32.ap[:, 1024:1152]),
                   r=[Sb32], w=["Sh%d" % hh])
                for blk in range(4):
                    op("pe", lambda e, blk=blk, v_bf=v_bf, scT=scT, B0=B0: e.matmul(
                        ps[B0][:, blk * 128:(blk + 1) * 128], lhsT=v_bf.ap[:, blk * 128:(blk + 1) * 128],
                        rhs=scT.ap[:, blk * 128:(blk + 1) * 128], start=True, stop=False), r=[v_bf, scT], w=[("ps", B0)])
                    for half in range(2):
                        c = blk * 2 + half
                        op("pe", lambda e, c=c, half=half, Sbf=Sbf, qt=qt, B0=B0: e.matmul(
                            ps[B0][:, c * 64:(c + 1) * 64], lhsT=Sbf.ap[:, c * 128:(c + 1) * 128],
                            rhs=qt.ap[:, c * 64:(c + 1) * 64], start=False, stop=(half == 1)),
                            r=[Sbf, qt], w=[("ps", B0)])
                head_out(B0, gsil, SM["hnw"] + l, hh, stat_bank=B1)
        else:
            for hh in range(NHH):
                op("dve", lambda e, hh=hh: e.memset(yv[:, hh, :], 0.0), w=[ykey(hh)])

        self.aoff = amark
        if "gdn" in self.mix_parts:
            self.gdn_heads(l, tg, yv, ykey, head_out)
        else:
            for hd in range(NGH):
                op("dve", lambda e, hd=hd: e.memset(yv[:, 4 + hd, :], 0.0), w=[ykey(4 + hd)])

        wout_d = self.wout_d
        for dc in range(NC):
            slot = self.wd_i % self.NWD
            self.wd_i += 1
            wt = self.wd_sb[slot]
            self.wload(wt[:, 0:NC, :].rearrange("p f d -> p (f d)"), wout_d[l, dc, :, :],
                       self.wout_s[l][dc, :, :] if self.use_scr else None, ("scr_out", l, dc),
                       ("wd", slot), f"wd{slot}", self.cur_ti == 0)
            par = dc % 2
            po = ps[4 + par]
            for ec in range(NC):
                op("pe", lambda e, wt=wt, ec=ec, po=po: e.matmul(po[:, :], lhsT=wt[:, ec, :], rhs=yv[:, ec, :],
                                                                 start=(ec == 0), stop=(ec == NC - 1)),
                   r=[("wd", slot), ykey(ec)], w=[("ps", 4 + par)])
            op("dve", lambda e, po=po, dc=dc: e.tensor_tensor(out=x_sb[:, dc, tsl], in0=po[:, :], in1=x_sb[:, dc, tsl],
                                                              op=ALU.add),
               r=[("ps", 4 + par), ("x", dc, tg)], w=[("x", dc, tg)])
        self.defer = None
        if os.environ.get("NOSCHED"):
            for it in outer_defer:
                self.s.add(*it)
        else:
            macros, members = [], []
            for idx, it in enumerate(outer_defer):
                if (it[0] == "pe" and members and outer_defer[members[-1][-1]][0] == "pe"
                        and members[-1][-1] == idx - 1 and outer_defer[idx - 1][3] == it[3]):
                    members[-1].append(idx)
                else:
                    members.append([idx])
            for mem in members:
                first = outer_defer[mem[0]]
                if len(mem) == 1:
                    macros.append(first)
                else:
                    rd = []
                    for i in mem:
                        rd.extend(outer_defer[i][2])
                    fns = [outer_defer[i][1] for i in mem]

                    def multi(e, fns=fns):
                        r = None
                        for f in fns:
                            r = f(e)
                        return r
                    macros.append((first[0], multi, rd, first[3], None, sum(op_cost("pe", f, None)[0] for f in fns)))
            order = list_schedule(macros)
            for mi in order:
                for i in members[mi]:
                    self.s.add(*outer_defer[i])

    def gdn_heads(self, l, tg, yv, ykey, head_out):
        op, ps, A = self.op, self.ps, self.A
        SM = self.SM
        small, con = self.small_sb, self.con_sb
        ones_bf, ident_bf, epsD = self.ones_bf, self.ident_bf, self.epsD_sb
        sq_sb = self.sq_sb
        bg = self.bg_sb
        BETA, NBETA, XA, GG, NGG, EGT, EGL, COEF, EGLB = range(9)
        c_ident = con[:, CON["ident"]:CON["ident"] + 128]
        c_tri = con[:, CON["tri"]:CON["tri"] + 128]
        c_ntri = con[:, CON["ntri"]:CON["ntri"] + 128]
        c_maskL = con[:, CON["maskL"]:CON["maskL"] + 128]
        c_maskU = con[:, CON["maskU"]:CON["maskU"] + 128]
        c_ones = con[:, CON["ones"]:CON["ones"] + 128]
        psT = ps[7][:, :].bitcast(BF16)
        ctail, Sg = self.ctail_sb, self.Sg_sb

        cb = [A(f"cb{j}", 515, F32) for j in range(3)]
        acc = [A(f"acc{j}", 512, F32) for j in range(3)]
        vT = A("vT", 512, BF16)
        rq = A("rq", 512, F32)
        rk = A("rk", 512, F32)
        qT = A("qT", 512, BF16)
        kT = A("kT", 512, BF16)
        ds = A("ds", 512, F32, like="cb0")
        dTi = A("dTi", 512, F32, like="cb1")
        EG = A("EG", 512, F32, like="cb2")
        gbc = A("gbc", 512, F32, like="acc2")
        H = []
        for par in range(2):
            H.append(dict(
                gsil=A(f"gsil{par}", 512, F32), qdT=A(f"qdT{par}", 512, BF16), kbg=A(f"kbg{par}", 512, BF16),
                kdec=A(f"kdec{par}", 512, BF16), kdecB=A(f"kdecB{par}", 512, BF16), vb=A(f"vb{par}", 512, BF16),
                M=A(f"M{par}", 512, BF16), MT=A(f"MT{par}", 512, BF16), attnT=A(f"attnT{par}", 512, BF16),
                lastc=A(f"lastc{par}", 8, F32)))
        Pb = [A("P0", 512, BF16), A("P1", 512, BF16)]
        PTb = [A("PT0", 512, BF16), A("PT1", 512, BF16)]
        Rb = [A("R0", 512, BF16), A("R1", 512, BF16)]
        u32 = A("u32", 512, F32)
        wT = A("wT", 512, BF16)
        vnew = A("vnew", 512, BF16)
        Sbf = [A("Sbf0", 128, BF16), A("Sbf1", 128, BF16)]
        sqB = A("sqB", 512, BF16)
        ctmp = A("ctmp", 512, F32) if CONV_ENG != "dve" else None
        rs, t1 = self._rs, self._t1

        def b4(t):
            return t.ap.rearrange("p (b t) -> p b t", t=128)

        def stageA(hd):
            h = H[hd % 2]
            gsil, qdT, kbg, kdec, kdecB, vb, M, MT, attnT, lastc = (h[k] for k in (
                "gsil", "qdT", "kbg", "kdec", "kdecB", "vb", "M", "MT", "attnT", "lastc"))
            wtA, kA = self.win_pair(l, 10 + 2 * hd)
            wtB, kB = self.win_pair(l, 10 + 2 * hd + 1)
            self.proj_fm(wtA, kA, 0, 0, tg)
            self.proj_fm(wtA, kA, 1, 1, tg)
            self.proj_fm(wtB, kB, 0, 2, tg)
            self.proj_fm(wtB, kB, 1, 3, tg)
            op("act", lambda e: e.activation(out=gsil.ap, in_=ps[3][:, :], func=AF.Silu), r=[("ps", 3)], w=[gsil])
            for j in range(3):
                cc = j * 8 + hd
                op("dve", lambda e, j=j, cc=cc: e.tensor_copy(out=cb[j].ap[:, 0:3], in_=ctail[:, l, cc, :]),
                   r=["ctail"], w=[cb[j]])
                op("act", lambda e, j=j: e.activation(out=cb[j].ap[:, 3:515], in_=ps[j][:, :], func=AF.Copy),
                   r=[("ps", j)], w=[cb[j]])
                op("dve", lambda e, j=j, cc=cc: e.tensor_copy(out=ctail[:, l, cc, :], in_=cb[j].ap[:, 512:515]),
                   r=[cb[j]], w=["ctail"])
                w0 = SM["convw"] + (l * 24 + cc) * 4
                if CONV_ENG == "dve":
                    op("dve", lambda e, j=j, w0=w0: e.tensor_scalar(out=acc[j].ap, in0=cb[j].ap[:, 3:515],
                                                                    scalar1=small[:, w0 + 3:w0 + 4], scalar2=None,
                                                                    op0=ALU.mult), r=[cb[j], "small"], w=[acc[j]])
                    for tap in (2, 1, 0):
                        op("dve", lambda e, j=j, w0=w0, tap=tap: e.scalar_tensor_tensor(
                            out=acc[j].ap, in0=cb[j].ap[:, tap:tap + 512], scalar=small[:, w0 + tap:w0 + tap + 1],
                            in1=acc[j].ap, op0=ALU.mult, op1=ALU.add), r=[cb[j], "small", acc[j]], w=[acc[j]])
                else:
                    op("pool", lambda e, j=j, w0=w0: e.tensor_scalar(out=acc[j].ap, in0=cb[j].ap[:, 3:515],
                                                                     scalar1=small[:, w0 + 3:w0 + 4], scalar2=None,
                                                                     op0=ALU.mult), r=[cb[j], "small"], w=[acc[j]])
                    for tap in (2, 1, 0):
                        op("pool", lambda e, j=j, w0=w0, tap=tap: e.tensor_scalar(
                            out=ctmp.ap, in0=cb[j].ap[:, tap:tap + 512], scalar1=small[:, w0 + tap:w0 + tap + 1],
                            scalar2=None, op0=ALU.mult), r=[cb[j], "small"], w=[ctmp])
                        op("pool", lambda e, j=j: e.tensor_tensor(out=acc[j].ap, in0=acc[j].ap, in1=ctmp.ap, op=ALU.add),
                           r=[acc[j], ctmp], w=[acc[j]])
            op("act", lambda e: e.activation(out=acc[0].ap, in_=acc[0].ap, func=AF.Silu), r=[acc[0]], w=[acc[0]])
            op("act", lambda e: e.activation(out=acc[1].ap, in_=acc[1].ap, func=AF.Silu), r=[acc[1]], w=[acc[1]])
            op("act", lambda e: e.activation(out=vT.ap, in_=acc[2].ap, func=AF.Silu), r=[acc[2]], w=[vT])
            for j, rr in ((0, rq), (1, rk)):
                sq = sq_sb[j]
                op("act", lambda e, j=j, sq=sq: e.activation(out=sq[:, :], in_=acc[j].ap, func=AF.Square),
                   r=[acc[j]], w=[("sq", j)])
                op("pe", lambda e, sq=sq: e.matmul(ps[7][:, :], lhsT=ones_bf[:, :], rhs=sq[:, :], start=True, stop=True),
                   r=[("sq", j), "ones"], w=[("ps", 7)])
                op("act", lambda e, rr=rr: e.activation(out=rr.ap, in_=ps[7][:, :], func=AF.Sqrt, bias=epsD[:, 2:3],
                                                        scale=1.0), r=[("ps", 7), "epsD"], w=[rr])
                op("dve", lambda e, rr=rr: e.reciprocal(out=rr.ap, in_=rr.ap), r=[rr], w=[rr])
            for blk in range(4):
                gcol = blk * 8 + hd
                op("dve", lambda e, blk=blk, gcol=gcol: e.tensor_scalar(
                    out=gbc.ap[:, blk * 128:(blk + 1) * 128], in0=c_ones, scalar1=bg[:, GG, gcol:gcol + 1], scalar2=None,
                    op0=ALU.mult), r=["con", ("bg", GG)], w=[gbc])
            for blk in range(4):
                bs = slice(blk * 128, (blk + 1) * 128)
                op("pe", lambda e, bs=bs: e.matmul(ps[0][:, bs], lhsT=gbc.ap[:, bs], rhs=c_tri, start=True, stop=True),
                   r=[gbc, "con"], w=[("ps", 0)])
            op("act", lambda e: e.activation(out=EG.ap, in_=ps[0][:, :], func=AF.Exp), r=[("ps", 0)], w=[EG])
            op("dve", lambda e: e.tensor_copy(out=lastc.ap, in_=EG.ap.rearrange("p (c t) -> p c t", t=64)[:, :, 63]),
               r=[EG], w=[lastc])
            op("dve", lambda e: e.scalar_tensor_tensor(out=acc[0].ap, in0=acc[0].ap, scalar=128.0 ** -0.5, in1=rq.ap,
                                                       op0=ALU.mult, op1=ALU.mult), r=[acc[0], rq], w=[acc[0]])
            op("act", lambda e: e.activation(out=qT.ap, in_=acc[0].ap, func=AF.Copy), r=[acc[0]], w=[qT])
            op("dve", lambda e: e.tensor_tensor(out=qdT.ap, in0=acc[0].ap, in1=EG.ap, op=ALU.mult),
               r=[acc[0], EG], w=[qdT])
            op("dve", lambda e: e.tensor_tensor(out=kT.ap, in0=acc[1].ap, in1=rk.ap, op=ALU.mult),
               r=[acc[1], rk], w=[kT])
            for blk in range(4):
                bs = slice(blk * 128, (blk + 1) * 128)
                op("pe", lambda e, bs=bs: e.transpose(out=psT[:, bs], in_=kT.ap[:, bs], identity=ident_bf[:, :]),
                   r=[kT, "identbf"], w=[("ps", 7)])
                op("pe", lambda e, blk=blk, bs=bs: e.transpose(out=psT[:, 512 + blk * 128:512 + (blk + 1) * 128],
                                                               in_=vT.ap[:, bs], identity=ident_bf[:, :]),
                   r=[vT, "identbf"], w=[("ps", 7)])
            for blk in range(4):
                bs = slice(blk * 128, (blk + 1) * 128)
                gcol = blk * 8 + hd
                op("dve", lambda e, bs=bs, gcol=gcol: e.tensor_scalar(out=kbg.ap[:, bs], in0=psT[:, bs],
                                                                      scalar1=bg[:, COEF, gcol:gcol + 1], scalar2=None,
                                                                      op0=ALU.mult), r=[("ps", 7), ("bg", COEF)], w=[kbg])
                op("dve", lambda e, bs=bs, gcol=gcol: e.tensor_scalar(out=kdec.ap[:, bs], in0=psT[:, bs],
                                                                      scalar1=bg[:, EGL, gcol:gcol + 1], scalar2=None,
                                                                      op0=ALU.mult), r=[("ps", 7), ("bg", EGL)], w=[kdec])
                op("dve", lambda e, bs=bs, gcol=gcol: e.tensor_scalar(out=kdecB.ap[:, bs], in0=psT[:, bs],
                                                                      scalar1=bg[:, EGLB, gcol:gcol + 1], scalar2=None,
                                                                      op0=ALU.mult), r=[("ps", 7), ("bg", EGLB)], w=[kdecB])
                op("dve", lambda e, blk=blk, bs=bs, gcol=gcol: e.tensor_scalar(
                    out=vb.ap[:, bs], in0=psT[:, 512 + blk * 128:512 + (blk + 1) * 128],
                    scalar1=bg[:, BETA, gcol:gcol + 1], scalar2=None, op0=ALU.mult),
                    r=[("ps", 7), ("bg", BETA)], w=[vb])
            for blk in range(4):
                bs = slice(blk * 128, (blk + 1) * 128)
                op("pe", lambda e, bs=bs: e.matmul(ps[1][:, bs], lhsT=c_tri, rhs=gbc.ap[:, bs], start=True, stop=False),
                   r=["con", gbc], w=[("ps", 1)])
                op("pe", lambda e, bs=bs: e.matmul(ps[1][:, bs], lhsT=gbc.ap[:, bs], rhs=c_ntri, start=False, stop=False),
                   r=["con", gbc], w=[("ps", 1)])
                op("pe", lambda e, bs=bs: e.matmul(ps[1][:, bs], lhsT=c_ident, rhs=c_maskL, start=False, stop=True),
                   r=["con"], w=[("ps", 1)])
            op("act", lambda e: e.activation(out=ds.ap, in_=ps[1][:, :], func=AF.Exp), r=[("ps", 1)], w=[ds])
            for blk in range(4):
                bs = slice(blk * 128, (blk + 1) * 128)
                op("pe", lambda e, bs=bs: e.matmul(ps[2][:, bs], lhsT=gbc.ap[:, bs], rhs=c_tri, start=True, stop=False),
                   r=["con", gbc], w=[("ps", 2)])
                op("pe", lambda e, bs=bs: e.matmul(ps[2][:, bs], lhsT=c_ntri, rhs=gbc.ap[:, bs], start=False, stop=False),
                   r=["con", gbc], w=[("ps", 2)])
                op("pe", lambda e, bs=bs: e.matmul(ps[2][:, bs], lhsT=c_ident, rhs=c_maskU, start=False, stop=True),
                   r=["con"], w=[("ps", 2)])
            op("act", lambda e: e.activation(out=dTi.ap, in_=ps[2][:, :], func=AF.Exp), r=[("ps", 2)], w=[dTi])
            for blk in range(4):
                bs = slice(blk * 128, (blk + 1) * 128)
                op("pe", lambda e, bs=bs: e.matmul(ps[3][:, bs], lhsT=kT.ap[:, bs], rhs=kT.ap[:, bs], start=True, stop=True),
                   r=[kT], w=[("ps", 3)])
                op("pe", lambda e, bs=bs: e.matmul(ps[0][:, bs], lhsT=kT.ap[:, bs], rhs=qT.ap[:, bs], start=True, stop=True),
                   r=[kT, qT], w=[("ps", 0)])
            for blk in range(4):
                bs = slice(blk * 128, (blk + 1) * 128)
                gcol = blk * 8 + hd
                op("dve", lambda e, bs=bs, gcol=gcol: e.scalar_tensor_tensor(
                    out=M.ap[:, bs], in0=ps[3][:, bs], scalar=bg[:, NBETA, gcol:gcol + 1], in1=ds.ap[:, bs],
                    op0=ALU.mult, op1=ALU.mult), r=[("ps", 3), ("bg", NBETA), ds], w=[M])
            op("dve", lambda e: e.tensor_tensor(out=attnT.ap, in0=ps[0][:, :], in1=dTi.ap, op=ALU.mult),
               r=[("ps", 0), dTi], w=[attnT])
            for blk in range(4):
                bs = slice(blk * 128, (blk + 1) * 128)
                op("pe", lambda e, bs=bs: e.transpose(out=psT[:, bs], in_=M.ap[:, bs], identity=ident_bf[:, :]),
                   r=[M, "identbf"], w=[("ps", 7)])
            op("act", lambda e: e.activation(out=MT.ap, in_=psT[:, 0:512], func=AF.Copy), r=[("ps", 7)], w=[MT])

        def stageBC(hd):
            h = H[hd % 2]
            gsil, qdT, kbg, kdec, kdecB, vb, M, MT, attnT, lastc = (h[k] for k in (
                "gsil", "qdT", "kbg", "kdec", "kdecB", "vb", "M", "MT", "attnT", "lastc"))
            P, PT = M, MT
            R = Rb[0]
            op("dve", lambda e, R=R: e.tensor_tensor(out=b4(R), in0=b4(MT),
                                                     in1=c_ident.unsqueeze(1).to_broadcast([128, 4, 128]), op=ALU.add),
               r=[MT, "con"], w=[R])
            for j in range(1, 6):
                Pn, PTn, Rn = Pb[j % 2], PTb[j % 2], Rb[j % 2]
                for blk in range(4):
                    bs = slice(blk * 128, (blk + 1) * 128)
                    op("pe", lambda e, bs=bs, P=P, PT=PT: e.matmul(ps[4][:, bs], lhsT=PT.ap[:, bs], rhs=P.ap[:, bs],
                                                                   start=True, stop=True), r=[P, PT], w=[("ps", 4)])
                if j < 5:
                    for blk in range(4):
                        bs = slice(blk * 128, (blk + 1) * 128)
                        op("pe", lambda e, bs=bs, P=P, PT=PT: e.matmul(ps[5][:, bs], lhsT=P.ap[:, bs], rhs=PT.ap[:, bs],
                                                                       start=True, stop=True), r=[P, PT], w=[("ps", 5)])
                op("act", lambda e, Pn=Pn: e.activation(out=Pn.ap, in_=ps[4][:, :], func=AF.Copy), r=[("ps", 4)], w=[Pn])
                if j < 5:
                    op("dve", lambda e, PTn=PTn: e.tensor_copy(out=PTn.ap, in_=ps[5][:, :]), r=[("ps", 5)], w=[PTn])
                for blk in range(4):
                    bs = slice(blk * 128, (blk + 1) * 128)
                    op("pe", lambda e, bs=bs, Pn=Pn, R=R: e.matmul(ps[6][:, bs], lhsT=Pn.ap[:, bs], rhs=R.ap[:, bs],
                                                                   start=True, stop=True), r=[Pn, R], w=[("ps", 6)])
                op("dve", lambda e, R=R, Rn=Rn: e.tensor_tensor(out=Rn.ap, in0=ps[6][:, :], in1=R.ap, op=ALU.add),
                   r=[("ps", 6), R], w=[Rn])
                P, PT, R = Pn, PTn, Rn
            for blk in range(4):
                bs = slice(blk * 128, (blk + 1) * 128)
                op("pe", lambda e, bs=bs, R=R: e.matmul(ps[4][:, bs], lhsT=R.ap[:, bs], rhs=vb.ap[:, bs], start=True, stop=True),
                   r=[R, vb], w=[("ps", 4)])
                op("pe", lambda e, bs=bs, R=R: e.matmul(ps[5][:, bs], lhsT=kbg.ap[:, bs], rhs=R.ap[:, bs], start=True, stop=True),
                   r=[R, kbg], w=[("ps", 5)])
            op("act", lambda e: e.activation(out=u32.ap, in_=ps[4][:, :], func=AF.Copy), r=[("ps", 4)], w=[u32])
            op("dve", lambda e: e.tensor_copy(out=wT.ap, in_=ps[5][:, :]), r=[("ps", 5)], w=[wT])
            op("act", lambda e: e.activation(out=Sbf[0].ap, in_=Sg[:, l, hd, :], func=AF.Copy), r=["Sg%d" % hd], w=[Sbf[0]])
            OB, WSB, KVB, STB = 4, 6, 5, 5
            for c in range(8):
                blk, half = c // 2, c % 2
                r0 = half * 64
                cs = slice(c * 64, (c + 1) * 64)
                bs = slice(blk * 128, (blk + 1) * 128)
                Sc, Sn = Sbf[c % 2], Sbf[(c + 1) % 2]
                wcols = slice((c % 4) * 128, (c % 4 + 1) * 128)
                op("pe", lambda e, bs=bs, Sc=Sc, wcols=wcols: e.matmul(
                    ps[WSB][:, wcols], lhsT=wT.ap[:, bs], rhs=Sc.ap, start=True, stop=True), r=[wT, Sc], w=[("ps", WSB)])
                if half == 0:
                    op("dve", lambda e, bs=bs, wcols=wcols: e.tensor_tensor(
                        out=vnew.ap[:, bs], in0=u32.ap[:, bs], in1=ps[WSB][:, wcols], op=ALU.subtract),
                        r=[u32, ("ps", WSB)], w=[vnew])
                else:
                    op("dve", lambda e, bs=bs, wcols=wcols: e.tensor_tensor(
                        out=vnew.ap[64:128, bs], in0=u32.ap[64:128, bs], in1=ps[WSB][64:128, wcols], op=ALU.subtract),
                        r=[u32, ("ps", WSB)], w=[vnew])
                op("pe", lambda e, r0=r0, bs=bs, cs=cs, blk=blk: e.matmul(
                    ps[OB][:, cs], lhsT=vnew.ap[:, bs], rhs=attnT.ap[:, blk * 128 + r0: blk * 128 + r0 + 64],
                    start=True, stop=False), r=[vnew, attnT], w=[("ps", OB)])
                op("pe", lambda e, cs=cs, Sc=Sc: e.matmul(ps[OB][:, cs], lhsT=Sc.ap, rhs=qdT.ap[:, cs],
                                                          start=False, stop=True), r=[Sc, qdT], w=[("ps", OB)])
                kd = kdec if half == 0 else kdecB
                op("pe", lambda e, bs=bs, c=c, kd=kd: e.matmul(
                    ps[KVB][:, (c % 4) * 128:(c % 4 + 1) * 128], lhsT=kd.ap[:, bs], rhs=vnew.ap[:, bs],
                    start=True, stop=True), r=[kd, vnew], w=[("ps", KVB)])
                if c < 7:
                    op("dve", lambda e, c=c, Sn=Sn: e.scalar_tensor_tensor(
                        out=Sn.ap, in0=Sg[:, l, hd, :], scalar=lastc.ap[:, c:c + 1],
                        in1=ps[KVB][:, (c % 4) * 128:(c % 4 + 1) * 128], op0=ALU.mult, op1=ALU.add),
                        r=["Sg%d" % hd, lastc, ("ps", KVB)], w=[Sn])
                op("dve", lambda e, c=c: e.scalar_tensor_tensor(
                    out=Sg[:, l, hd, :], in0=Sg[:, l, hd, :], scalar=lastc.ap[:, c:c + 1],
                    in1=ps[KVB][:, (c % 4) * 128:(c % 4 + 1) * 128], op0=ALU.mult, op1=ALU.add),
                    r=["Sg%d" % hd, lastc, ("ps", KVB)], w=["Sg%d" % hd])
            op("act", lambda e: e.activation(out=sqB.ap, in_=ps[OB][:, :], func=AF.Square), r=[("ps", OB)], w=[sqB])
            op("pe", lambda e: e.matmul(ps[STB][:, :], lhsT=ones_bf[:, :], rhs=sqB.ap, start=True, stop=True),
               r=[sqB, "ones"], w=[("ps", STB)])
            op("act", lambda e: e.activation(out=rs.ap, in_=ps[STB][:, :], func=AF.Sqrt, bias=epsD[:, 1:2], scale=1.0),
               r=[("ps", STB), "epsD"], w=[rs])
            op("dve", lambda e: e.reciprocal(out=rs.ap, in_=rs.ap), r=[rs], w=[rs])
            nw = SM["gnw"] + l
            op("dve", lambda e: e.scalar_tensor_tensor(out=t1.ap, in0=ps[OB][:, :], scalar=small[:, nw:nw + 1], in1=rs.ap,
                                                       op0=ALU.mult, op1=ALU.mult), r=[("ps", OB), "small", rs], w=[t1])
            op("dve", lambda e: e.tensor_tensor(out=yv[:, 4 + hd, :], in0=t1.ap, in1=gsil.ap, op=ALU.mult),
               r=[t1, gsil], w=[ykey(4 + hd)])

        for hd in range(NGH):
            self.tag = "A%d" % hd
            stageA(hd)
            self.tag = "BC%d" % hd
            stageBC(hd)
        self.tag = "out"


def _prep_wgu(inp, L):
    wgu = np.empty((L, 2, NF, 128, 2, NC, 128), np.float32)
    for which, (g, u) in enumerate((("ffn1_w_gate", "ffn1_w_up"), ("ffn2_w_gate", "ffn2_w_up"))):
        for gu, name in enumerate((g, u)):
            w = np.asarray(inp[name])[:L]
            w = w.reshape(L, NC, 128, NF, 128)
            wgu[:, which, :, :, gu, :, :] = w.transpose(0, 3, 2, 1, 4)
    return wgu.reshape(L, 2, NF, 128, 2 * NC * 128)


def _prep_wd(inp, L, FG):
    NFG = NF // FG
    wd = np.empty((L, 2, NC, NFG, 128, FG, 128), np.float32)
    for which, name in enumerate(("ffn1_w_down", "ffn2_w_down")):
        w = np.asarray(inp[name])[:L]
        w = w.reshape(L, NFG, FG, 128, NC, 128)
        wd[:, which] = w.transpose(0, 4, 1, 3, 2, 5)
    return wd.reshape(L, 2, NC, NFG, 128, FG * 128)


def _prep_norms(inp, L):
    cols = []
    for l in range(L):
        for name in ("norm_ffn1", "norm_mix", "norm_ffn2"):
            cols.append(np.asarray(inp[name])[l])
    cols.append(np.asarray(inp["norm_final"]))
    a = np.stack(cols, 0).reshape(3 * L + 1, NC, 128)
    return np.ascontiguousarray(a.transpose(2, 0, 1).reshape(128, (3 * L + 1) * NC)).astype(np.float32)


def _prep_win(inp, L):
    w = np.asarray(inp["w_in"])[:L]
    win = np.empty((L, NSLAB // 2, 128, 2, NC, 128), np.float32)
    for sidx, c0 in enumerate(WIN_SLABS):
        blk = w[:, :, c0:c0 + 128].reshape(L, NC, 128, 128)
        win[:, sidx // 2, :, sidx % 2, :, :] = blk.transpose(0, 2, 1, 3)
    bg = w[:, :, 6144:6160].reshape(L, NC, 128, 16).transpose(2, 0, 1, 3)
    return win.reshape(L, NSLAB // 2, 128, 2 * NC * 128), np.ascontiguousarray(bg).reshape(128, L * NC * 16)


def _prep_wout(inp, L):
    w = np.asarray(inp["w_out"])[:L].reshape(L, NC, 128, NC, 128)
    return np.ascontiguousarray(w.transpose(0, 3, 2, 1, 4)).reshape(L, NC, 128, NC * 128)


def _prep_small(inp, L):
    SM = _small_map(L)
    sm = np.zeros((128, SM["_n"]), np.float32)
    lbl = np.asarray(inp["lb_logits"]).reshape(LBL, 4, 128)
    sm[:, SM["lbl"]:SM["lbl"] + 4 * LBL] = lbl.transpose(2, 1, 0).reshape(128, 4 * LBL)
    sm[:, SM["hnw"]:SM["hnw"] + L] = np.asarray(inp["hgrn_norm_w"])[:L].T
    sm[:, SM["gnw"]:SM["gnw"] + L] = np.asarray(inp["gdn_norm_w"])[:L].T
    psc = np.asarray(inp["pool_scale"])[:L].reshape(L, 4, 128)
    sm[:, SM["psc"]:SM["psc"] + 4 * L] = psc.transpose(2, 0, 1).reshape(128, 4 * L)
    cw = np.asarray(inp["gdn_conv_w"])[:L].reshape(L, 4, 24, 128)
    sm[:, SM["convw"]:SM["convw"] + L * 96] = cw.transpose(3, 0, 2, 1).reshape(128, L * 96)
    for nm, key in (("alog", "gdn_a_log"), ("dtb", "gdn_dt_bias")):
        a = np.asarray(inp[key])[:L]
        rep = np.broadcast_to(a[None, :, None, :], (128, L, 4, 8)).reshape(128, L * 32)
        sm[:, SM[nm]:SM[nm] + L * 32] = rep
    return sm


def _consts():
    c = np.zeros((128, CON["_n"]), np.float32)
    i = np.arange(128)
    same = (i[:, None] // 64) == (i[None, :] // 64)
    c[:, CON["ident"]:CON["ident"] + 128] = np.eye(128)
    c[:, CON["tri"]:CON["tri"] + 128] = same & (i[:, None] <= i[None, :])
    c[:, CON["triS"]:CON["triS"] + 128] = same & (i[:, None] > i[None, :])
    c[:, CON["maskL"]:CON["maskL"] + 128] = np.where(same & (i[:, None] > i[None, :]), 0.0, NEG)
    c[:, CON["maskU"]:CON["maskU"] + 128] = np.where(same & (i[None, :] >= i[:, None]), 0.0, NEG)
    c[:, CON["m01U"]:CON["m01U"] + 128] = same & (i[None, :] >= i[:, None])
    c[:, CON["ones"]:CON["ones"] + 128] = 1.0
    c[:64, CON["rmA"]] = 1.0
    c[:, CON["ntri"]:CON["ntri"] + 128] = -1.0 * (same & (i[:, None] <= i[None, :]))
    c[64:, CON["rmB"]] = 1.0
    t = np.arange(16)
    for gi, win in enumerate((2, 4, 8, 16)):
        c[:, CON["invc"] + gi * 16:CON["invc"] + (gi + 1) * 16] = 1.0 / np.minimum(t + 1, win)
    return c


def prep_shared(inputs, depth, bld):
    win, bgw = _prep_win(inputs, depth)
    pw = np.asarray(inputs["pool_w"])[:depth]
    poolw = np.ascontiguousarray(pw.transpose(2, 0, 1, 3)).reshape(128, depth * 4 * 128)
    return {
        "wgu": _prep_wgu(inputs, depth),
        "wd": _prep_wd(inputs, depth, bld.FG),
        "norms": _prep_norms(inputs, depth),
        "win": win,
        "bgw": bgw,
        "wout": _prep_wout(inputs, depth),
        "poolw": poolw,
        "small": _prep_small(inputs, depth),
        "consts": _consts(),
    }


def kernel(**inputs):
    x = np.asarray(inputs["x"])
    B = x.shape[0]
    bld = Builder()
    nc = bld.build()
    shared = prep_shared(inputs, DEPTH, bld)
    in_maps = []
    for b in range(B):
        m = dict(shared)
        m["xT"] = np.ascontiguousarray(x[b].T)
        in_maps.append(m)
    res = run_bass_kernel_spmd(nc, in_maps, core_ids=list(range(B)))
    out = np.stack([np.ascontiguousarray(r["yT"].T) for r in res.results], 0)
    return out.astype(np.float32)
```
